# Optimizing a Trainium2 kernel written in Bass

```python
import jax, jax.numpy as jnp
from jax import lax
import numpy as np

D_MODEL = 1024
BATCH = 4
SEQ = 4096
DEPTH = 2

GRID_W = 64
CTX_LEN = 256
N_MIXERS = 2
N_POOL_LAYERS = (DEPTH + N_MIXERS - 1) // N_MIXERS
N_ATTN_LAYERS = DEPTH // N_MIXERS
POOL_WINDOWS = (2, 4, 8, 16)
N_POOL_GROUPS = len(POOL_WINDOWS)
POOL_GROUP_DIM = D_MODEL // N_POOL_GROUPS
HEAD_DIM = 64
N_HEADS = D_MODEL // HEAD_DIM
N_KV_HEADS = 2
GQA_GROUP = N_HEADS // N_KV_HEADS
Q_DIM = N_HEADS * HEAD_DIM
KV_DIM = N_KV_HEADS * HEAD_DIM
WINDOW = 128
BLOCK = 128
ROPE_BASE = 10000.0
AXIS_ROT = HEAD_DIM // 2
D_FF = 4 * D_MODEL
N_MOD = 6
EPS = 1e-6
NEG = -1e30

kernel_name = 'hybrid_pool_swa_dit_block'


def rmsnorm(x, g):
    xf = x.astype(jnp.float32)
    y = xf * lax.rsqrt(jnp.mean(xf * xf, axis=-1, keepdims=True) + EPS)
    return (y * g.astype(jnp.float32)).astype(x.dtype)


def modulate(h, shift, scale):
    return h * (1 + scale) + shift


def pool_minus_self(u):
    B, L, G, C = u.shape
    uf = u.astype(jnp.float32)
    cs = jnp.concatenate([jnp.zeros((B, 1, G, C), jnp.float32), lax.cumsum(uf, axis=1)], axis=1)
    t = jnp.arange(L)
    outs = []
    for g, w in enumerate(POOL_WINDOWS):
        lo = jnp.clip(t - w // 2, 0, L)
        hi = jnp.clip(t + w // 2, 0, L)
        cnt = (hi - lo).astype(jnp.float32)
        csg = cs[:, :, g]
        s = jnp.take(csg, hi, axis=1) - jnp.take(csg, lo, axis=1)
        outs.append(s / cnt[None, :, None])
    pooled = jnp.stack(outs, axis=2)
    return (pooled - uf).astype(u.dtype)


def pool_mixer(h, w_in, w_grp, scale, w_out):
    B, L, _ = h.shape
    u = (h @ w_in).reshape(B, L, N_POOL_GROUPS, POOL_GROUP_DIM)
    d = pool_minus_self(u)
    y = jnp.einsum('blgc,gce->blge', d, w_grp).reshape(B, L, D_MODEL) * scale
    return y @ w_out


def axial_angles(L):
    rows = L // GRID_W
    row = jnp.repeat(jnp.arange(rows), GRID_W).astype(jnp.float32)
    col = jnp.tile(jnp.arange(GRID_W), rows).astype(jnp.float32)
    inv = ROPE_BASE ** (-jnp.arange(0, AXIS_ROT, 2, dtype=jnp.float32) / AXIS_ROT)
    ang_r = row[:, None] * inv[None]
    ang_c = col[:, None] * inv[None]
    return (jnp.cos(ang_r), jnp.sin(ang_r), jnp.cos(ang_c), jnp.sin(ang_c))


def rotate_axis(u, cos, sin):
    r = u.shape[-1] // 2
    u1, u2 = u[..., :r], u[..., r:]
    cos = cos[:, None, :].astype(u.dtype)
    sin = sin[:, None, :].astype(u.dtype)
    return jnp.concatenate([u1 * cos - u2 * sin, u1 * sin + u2 * cos], axis=-1)


def rope_2d(t, ang):
    cos_r, sin_r, cos_c, sin_c = ang
    return jnp.concatenate([rotate_axis(t[..., :AXIS_ROT], cos_r, sin_r),
                            rotate_axis(t[..., AXIS_ROT:], cos_c, sin_c)], axis=-1)


def window_attention(h, hc, w_qkv, sink, w_o, ang, with_ctx_out):
    B, L, _ = h.shape
    ctx_len = hc.shape[1]
    nb = L // BLOCK
    qscale = HEAD_DIM ** -0.5
    qkv = h @ w_qkv
    q = qkv[..., :Q_DIM].reshape(B, L, N_HEADS, HEAD_DIM)
    k = qkv[..., Q_DIM:Q_DIM + KV_DIM].reshape(B, L, N_KV_HEADS, HEAD_DIM)
    v = qkv[..., Q_DIM + KV_DIM:].reshape(B, L, N_KV_HEADS, HEAD_DIM)
    q = rope_2d(q, ang) * qscale
    k = rope_2d(k, ang)
    kv_c = hc @ w_qkv[:, Q_DIM:]
    kc = kv_c[..., :KV_DIM].reshape(B, ctx_len, N_KV_HEADS, HEAD_DIM)
    vc = kv_c[..., KV_DIM:].reshape(B, ctx_len, N_KV_HEADS, HEAD_DIM)

    qb = q.reshape(B, nb, BLOCK, N_KV_HEADS, GQA_GROUP, HEAD_DIM)

    def band(t):
        tp = jnp.pad(t, ((0, 0), (BLOCK, BLOCK), (0, 0), (0, 0)))
        tp = tp.reshape(B, nb + 2, BLOCK, N_KV_HEADS, HEAD_DIM)
        return jnp.concatenate([tp[:, :-2], tp[:, 1:-1], tp[:, 2:]], axis=2)

    kw, vw = band(k), band(v)
    s_win = jnp.einsum('bnqhgd,bnkhd->bnhgqk', qb, kw).astype(jnp.float32)
    qpos = jnp.arange(L).reshape(nb, BLOCK)
    kpos = (jnp.arange(nb) * BLOCK - BLOCK)[:, None] + jnp.arange(3 * BLOCK)[None, :]
    valid = ((kpos >= 0) & (kpos < L))[:, None, :] & (jnp.abs(qpos[:, :, None] - kpos[:, None, :]) <= WINDOW)
    s_win = jnp.where(valid[None, :, None, None], s_win, NEG)
    s_ctx = jnp.einsum('bnqhgd,bchd->bnhgqc', qb, kc).astype(jnp.float32)
    sink_f = sink.astype(jnp.float32).reshape(1, 1, N_KV_HEADS, GQA_GROUP, 1, 1)
    s_sink = jnp.broadcast_to(sink_f, s_win.shape[:-1] + (1,))
    p = jax.nn.softmax(jnp.concatenate([s_win, s_ctx, s_sink], axis=-1), axis=-1)
    p_win = p[..., :3 * BLOCK].astype(v.dtype)
    p_ctx = p[..., 3 * BLOCK:3 * BLOCK + ctx_len].astype(v.dtype)
    o = (jnp.einsum('bnhgqk,bnkhd->bnqhgd', p_win, vw)
         + jnp.einsum('bnhgqc,bchd->bnqhgd', p_ctx, vc))
    y = o.reshape(B, L, Q_DIM) @ w_o

    yc = None
    if with_ctx_out:
        qc = (hc @ w_qkv[:, :Q_DIM]).reshape(B, ctx_len, N_KV_HEADS, GQA_GROUP, HEAD_DIM) * qscale
        sc = jnp.einsum('bqhgd,bchd->bhgqc', qc, kc).astype(jnp.float32)
        sc_sink = jnp.broadcast_to(sink.astype(jnp.float32).reshape(1, N_KV_HEADS, GQA_GROUP, 1, 1),
                                   sc.shape[:-1] + (1,))
        pc = jax.nn.softmax(jnp.concatenate([sc, sc_sink], axis=-1), axis=-1)
        oc = jnp.einsum('bhgqc,bchd->bqhgd', pc[..., :ctx_len].astype(vc.dtype), vc)
        yc = oc.reshape(B, ctx_len, Q_DIM) @ w_o
    return y, yc


def sq_relu_mlp(h, w1, w2):
    return jnp.square(jax.nn.relu(h @ w1)) @ w2


def setup_inputs(seed: int = 0) -> dict:
    key = jax.random.key(seed)
    ks = jax.random.split(key, 20)
    nrm = jax.random.normal
    f32 = jnp.float32
    return {
        'x': nrm(ks[0], (BATCH, SEQ, D_MODEL), f32),
        'c': nrm(ks[1], (BATCH, D_MODEL), f32),
        'ctx': nrm(ks[2], (BATCH, CTX_LEN, D_MODEL), f32),
        'c_ctx': nrm(ks[3], (D_MODEL,), f32),
        'ada_w': nrm(ks[4], (DEPTH, D_MODEL, N_MOD * D_MODEL), f32) * (0.5 * D_MODEL ** -0.5),
        'ada_b': nrm(ks[5], (DEPTH, N_MOD * D_MODEL), f32) * 0.01,
        'norm_mix_g': 1.0 + 0.05 * nrm(ks[6], (DEPTH, D_MODEL), f32),
        'norm_mlp_g': 1.0 + 0.05 * nrm(ks[7], (DEPTH, D_MODEL), f32),
        'pool_w_in': nrm(ks[8], (N_POOL_LAYERS, D_MODEL, D_MODEL), f32) * D_MODEL ** -0.5,
        'pool_w_grp': nrm(ks[9], (N_POOL_LAYERS, N_POOL_GROUPS, POOL_GROUP_DIM, POOL_GROUP_DIM), f32) * POOL_GROUP_DIM ** -0.5,
        'pool_scale': 1.0 + 0.1 * nrm(ks[10], (N_POOL_LAYERS, D_MODEL), f32),
        'pool_w_out': nrm(ks[11], (N_POOL_LAYERS, D_MODEL, D_MODEL), f32) * D_MODEL ** -0.5,
        'attn_w_qkv': nrm(ks[12], (N_ATTN_LAYERS, D_MODEL, Q_DIM + 2 * KV_DIM), f32) * D_MODEL ** -0.5,
        'attn_sink': 0.5 * nrm(ks[13], (N_ATTN_LAYERS, N_HEADS), f32),
        'attn_w_o': nrm(ks[14], (N_ATTN_LAYERS, Q_DIM, D_MODEL), f32) * Q_DIM ** -0.5,
        'mlp_w1': nrm(ks[15], (DEPTH, D_MODEL, D_FF), f32) * D_MODEL ** -0.5,
        'mlp_w2': nrm(ks[16], (DEPTH, D_FF, D_MODEL), f32) * D_FF ** -0.5,
        'final_g': 1.0 + 0.05 * nrm(ks[17], (D_MODEL,), f32),
    }


def reference(x, c, ctx, c_ctx, ada_w, ada_b, norm_mix_g, norm_mlp_g, pool_w_in, pool_w_grp,
              pool_scale, pool_w_out, attn_w_qkv, attn_sink, attn_w_o, mlp_w1, mlp_w2, final_g):
    L = x.shape[1]
    ang = axial_angles(L)
    silu_c = jax.nn.silu(c)
    silu_cc = jax.nn.silu(c_ctx)[None]
    for i in range(DEPTH):
        last = i == DEPTH - 1
        kind = i % N_MIXERS
        j = i // N_MIXERS
        mod = (silu_c @ ada_w[i] + ada_b[i])[:, None, :]
        mod_c = (silu_cc @ ada_w[i] + ada_b[i])[:, None, :]
        sh1, sc1, g1, sh2, sc2, g2 = jnp.split(mod, N_MOD, axis=-1)
        csh1, csc1, cg1, csh2, csc2, cg2 = jnp.split(mod_c, N_MOD, axis=-1)

        h = modulate(rmsnorm(x, norm_mix_g[i]), sh1, sc1)
        if kind == 0:
            y = pool_mixer(h, pool_w_in[j], pool_w_grp[j], pool_scale[j], pool_w_out[j])
            if not last:
                hc = modulate(rmsnorm(ctx, norm_mix_g[i]), csh1, csc1)
                yc = pool_mixer(hc, pool_w_in[j], pool_w_grp[j], pool_scale[j], pool_w_out[j])
        else:
            hc = modulate(rmsnorm(ctx, norm_mix_g[i]), csh1, csc1)
            y, yc = window_attention(h, hc, attn_w_qkv[j], attn_sink[j], attn_w_o[j], ang, not last)

        x = x + g1 * y
        h = modulate(rmsnorm(x, norm_mlp_g[i]), sh2, sc2)
        x = x + g2 * sq_relu_mlp(h, mlp_w1[i], mlp_w2[i])

        if not last:
            ctx = ctx + cg1 * yc
            hc = modulate(rmsnorm(ctx, norm_mlp_g[i]), csh2, csc2)
            ctx = ctx + cg2 * sq_relu_mlp(hc, mlp_w1[i], mlp_w2[i])
    return rmsnorm(x, final_g)
```

```python
import numpy as np
from contextlib import ExitStack
import concourse.bass as bass
import concourse.mybir as mybir
from concourse.bass_utils import run_bass_kernel_spmd

F32 = mybir.dt.float32
BF16 = mybir.dt.bfloat16
AF = mybir.ActivationFunctionType
ALU = mybir.AluOpType
ESZ = {F32: 4, BF16: 2}

D = 1024
NCH = 8
SEQ = 4096
NB = 4
NT = 2176
NBLK = 17
NCTX = 256
DFF = 4096
NHC = 32
EPS = 1e-6
ARENA_BYTES = 212800
PSUM_BANK = 2048
STRICT_SAME_ENGINE = True


class _Op:
    __slots__ = ("id", "eng", "chan", "fn", "deps", "signal", "waits", "pos", "inc", "sigval")


class Sched:
    BK = 2048

    def __init__(self, nc, psum_names):
        self.nc = nc
        self.ops = []
        self.streams = {k: [] for k in ("pe", "act", "dve", "pool", "sp")}
        self.chan_ops = {}
        self.buckets = {}
        self.psum_names = psum_names

    def rects(self, ap):
        t = ap.tensor
        name = t.name
        if name in self.psum_names:
            b = self.psum_names[name]
            return "ps", [(0, 128, b * PSUM_BANK, (b + 1) * PSUM_BANK)]
        if name != "arena":
            return None, []
        esz = ESZ[ap.dtype]
        dims = [tuple(d) for d in ap.ap]
        pstride, pcount = dims[0]
        off = int(ap.offset)
        if pstride > 0:
            p0 = off // pstride
            foff = off % pstride
        else:
            p0 = 0
            foff = off
        p1 = p0 + pcount
        free = dims[1:]
        if not free:
            return "sb", [(p0, p1, foff * esz, (foff + 1) * esz)]
        ls, ln = free[-1]
        run = ((ln - 1) * abs(ls) + 1)
        outer = free[:-1]
        tot = 1
        for s, n in outer:
            tot *= n
        res = []
        if tot <= 64:
            idxs = [0]
            for s, n in outer:
                idxs = [b + s * i for b in idxs for i in range(n)]
            for b in idxs:
                st = foff + b
                res.append((p0, p1, st * esz, (st + run) * esz))
        else:
            mx = foff + sum(s * (n - 1) for s, n in outer)
            res.append((p0, p1, foff * esz, (mx + run) * esz))
        return "sb", res

    def _query_insert(self, op, accs):
        deps = {}
        newrecs = []
        for space, r, isw in accs:
            p0, p1, b0, b1 = r
            seen = set()
            for bk in range(b0 // self.BK, (b1 - 1) // self.BK + 1):
                lst = self.buckets.get((space, bk))
                if not lst:
                    continue
                for rec in lst:
                    if not rec[7] or id(rec) in seen:
                        continue
                    seen.add(id(rec))
                    if rec[2] < b1 and b0 < rec[3] and rec[0] < p1 and p0 < rec[1]:
                        if rec[5] or isw:
                            kind = "RAW" if (rec[5] and not isw) else "W"
                            prev = deps.get(rec[4])
                            if prev is None or kind == "RAW":
                                deps[rec[4]] = kind
                        if isw and p0 <= rec[0] and rec[1] <= p1 and b0 <= rec[2] and rec[3] <= b1:
                            rec[7] = False
                        elif (not isw) and (not rec[5]) and rec[6] == op.chan and rec[0] == p0 and rec[1] == p1 \
                                and rec[2] == b0 and rec[3] == b1:
                            rec[7] = False
            newrecs.append((space, [p0, p1, b0, b1, op.id, isw, op.chan, True]))
        for space, rec in newrecs:
            for bk in range(rec[2] // self.BK, (rec[3] - 1) // self.BK + 1):
                lst = self.buckets.setdefault((space, bk), [])
                lst.append(rec)
                if len(lst) > 96:
                    lst[:] = [x for x in lst if x[7]]
        deps.pop(op.id, None)
        return deps

    def op(self, eng, fn, reads=(), writes=(), chan=None, inc=1, extra_deps=()):
        o = _Op()
        o.id = len(self.ops)
        o.eng = eng
        o.chan = chan or eng
        o.fn = fn
        o.inc = inc
        o.signal = chan is not None
        o.waits = []
        o.sigval = 0
        accs = []
        for ap in reads:
            if ap is None or isinstance(ap, (int, float)):
                continue
            sp, rs = self.rects(ap)
            for r in rs:
                accs.append((sp, r, sp == "ps"))
        for ap in writes:
            sp, rs = self.rects(ap)
            for r in rs:
                accs.append((sp, r, True))
        o.deps = self._query_insert(o, accs)
        for d in extra_deps:
            o.deps[d.id] = "RAW"
        self.ops.append(o)
        self.streams[eng].append(o)
        lst = self.chan_ops.setdefault(o.chan, [])
        lst.append(o)
        o.pos = len(lst)
        return o

    def plan(self):
        known = {e: {} for e in self.streams}
        snap = {}
        for o in self.ops:
            K = known[o.eng]
            need = {}
            for d, kind in o.deps.items():
                dop = self.ops[d]
                if dop.chan == o.chan and dop.chan == o.eng:
                    if o.eng == "pe" or (kind != "RAW" and not STRICT_SAME_ENGINE):
                        continue
                if need.get(dop.chan, 0) < dop.pos:
                    need[dop.chan] = dop.pos
            waits = []
            for chan, pos in need.items():
                if K.get(chan, 0) >= pos:
                    continue
                waits.append((chan, pos))
            for chan, pos in waits:
                tgt = self.chan_ops[chan][pos - 1]
                tgt.signal = True
                if K.get(chan, 0) < pos:
                    K[chan] = pos
                for c2, p2 in snap[tgt.id].items():
                    if K.get(c2, 0) < p2:
                        K[c2] = p2
            o.waits = waits
            s = dict(K)
            s[o.chan] = o.pos
            snap[o.id] = s
        for chan, lst in self.chan_ops.items():
            cnt = 0
            for o in lst:
                if o.signal:
                    cnt += o.inc
                o.sigval = cnt

    def emit(self):
        nc = self.nc
        self.plan()
        with ExitStack() as es:
            sems = {}
            for chan in self.chan_ops:
                if any(o.signal for o in self.chan_ops[chan]):
                    sems[chan] = es.enter_context(nc.semaphore("s_" + chan.replace(":", "_")))
            block = es.enter_context(nc.Block())

            def run(stream):
                def f(e):
                    for o in self.streams[stream]:
                        for chan, pos in o.waits:
                            e.wait_ge(sems[chan], self.chan_ops[chan][pos - 1].sigval)
                        ins = o.fn(e) if o.fn is not None else None
                        if o.signal and ins is not None:
                            ins.then_inc(sems[o.chan], o.inc)
                return f

            block.tensor(run("pe"))
            block.scalar(run("act"))
            block.vector(run("dve"))
            block.gpsimd(run("pool"))
            block.sync(run("sp"))


class Prog:
    def __init__(self, nc):
        self.nc = nc
        self.psum = []
        names = {}
        for b in range(8):
            t = nc.alloc_psum_tensor("psb%d" % b, [128, 512], F32)
            names["psb%d" % b] = b
            self.psum.append(t)
        self.S = Sched(nc, names)
        self.arena = nc.alloc_sbuf_tensor("arena", [128, ARENA_BYTES // 2], BF16)
        self.bank_rr = 0
        self.dma_n = 0

    def view(self, off, shape, dt):
        n = 1
        for s in shape:
            n *= s
        nb = n * ESZ[dt]
        assert off % 4 == 0 and off + nb <= ARENA_BYTES, (off, nb)
        ap = self.arena[:, off // 2:(off + nb) // 2]
        if dt != BF16:
            ap = ap.bitcast(dt)
        if len(shape) > 1:
            names = "abcdef"[:len(shape)]
            pat = "p (%s) -> p %s" % (" ".join(names), " ".join(names))
            ap = ap.rearrange(pat, **{names[i]: shape[i] for i in range(len(shape) - 1)})
        return ap

    def bank(self):
        b = self.bank_rr
        self.bank_rr = (b + 1) % 7
        return self.psum[b]

    def mm(self, out, lhsT, rhs, start=True, stop=True):
        return self.S.op("pe", lambda e: e.matmul(out, lhsT, rhs, start=start, stop=stop),
                         reads=[lhsT, rhs], writes=[out])

    def act(self, out, in_, func, scale=1.0, bias=0.0, eng="act"):
        rd = [in_]
        if not isinstance(scale, (int, float)):
            rd.append(scale)
        if not isinstance(bias, (int, float)):
            rd.append(bias)
        return self.S.op(eng, lambda e: e.activation(out, in_, func, bias=bias, scale=scale), reads=rd, writes=[out])

    def tt(self, out, in0, in1, op, eng="dve"):
        return self.S.op(eng, lambda e: e.tensor_tensor(out, in0, in1, op), reads=[in0, in1], writes=[out])

    def ts(self, out, in0, s1, s2, op0, op1=None, eng="dve"):
        rd = [in0]
        for s in (s1, s2):
            if s is not None and not isinstance(s, (int, float)):
                rd.append(s)
        if op1 is None:
            return self.S.op(eng, lambda e: e.tensor_scalar(out, in0, s1, None, op0), reads=rd, writes=[out])
        return self.S.op(eng, lambda e: e.tensor_scalar(out, in0, s1, s2, op0, op1), reads=rd, writes=[out])

    def stt(self, out, in0, scalar, in1, op0, op1, eng="dve"):
        rd = [in0, in1]
        if not isinstance(scalar, (int, float)):
            rd.append(scalar)
        return self.S.op(eng, lambda e: e.scalar_tensor_tensor(out, in0, scalar, in1, op0, op1), reads=rd, writes=[out])

    def copy(self, out, in_, eng="dve"):
        return self.S.op(eng, lambda e: e.tensor_copy(out, in_), reads=[in_], writes=[out])

    def memset(self, out, val, eng="dve"):
        return self.S.op(eng, lambda e: e.memset(out, val), reads=[], writes=[out])

    def dma(self, q, out, in_, slot=None):
        if slot is None:
            slot = "d%d" % self.dma_n
            self.dma_n += 1
        return self.S.op(q, lambda e: e.dma_start(out=out, in_=in_), reads=[in_], writes=[out],
                         chan="dma:" + slot, inc=16)


TILES = [(0, 4), (4, 7), (7, 10), (10, 13), (13, 17)]
OFF_X = 0
OFF_CX = 69632
OFF_H = 77824
OFF_SQ = 94208
OFF_TMP = 102400
OFF_RSTD = 106496
OFF_CONST = 110592
OFF_ADA = 115200
OFF_SCR = 123392


def build(mode="ALL"):
    nc = bass.Bass("TRN2", target_bir_lowering=False)
    P = Prog(nc)
    S = P.S
    do0 = mode in ("ALL", "L0")
    do1 = mode in ("ALL", "L1")

    def din(name, shape):
        return nc.dram_tensor(name, shape, F32, kind="ExternalInput").ap()

    def dout(name, shape):
        return nc.dram_tensor(name, shape, F32, kind="ExternalOutput").ap()

    d_xT = din("xT", [128, NCH, NT])
    d_ctxT = din("ctxT", [128, NCH, NCTX])
    d_cT = din("cT", [128, NCH * 2])
    d_small = din("small", [128, 16 + 256 + 48 + 96 + 16 + 128])
    d_ada = din("ada_r", [2, 24, 128, 2048])
    d_w1 = din("w1_r", [2, 16, 128, 2048])
    d_w2 = din("w2_r", [2, 8, 128, 4096])
    if do0:
        d_win = din("win_r", [128, 8192])
        d_wout = din("wout_r", [128, 8192])
        d_wgrp = din("wgrp_r", [128, 2048])
    if do1:
        d_wqkv = din("wqkv_r", [128, 8 * 1280])
        d_wo = din("wo_r", [128, 8192])
        d_rope = din("rope", [128, 2 * NT])
        d_mask = din("masks", [128, 384])
        d_out = dout("outT", [128, NCH, NT])
    else:
        d_x1 = dout("x1T", [128, NCH, NT])
        d_c1 = dout("c1T", [128, NCH, NCTX])

    X = P.view(OFF_X, [NCH, NT], F32)
    CX = P.view(OFF_CX, [NCH, NCTX], F32)
    co = [OFF_CONST]

    def calloc(shape, dt):
        n = 1
        for s_ in shape:
            n *= s_
        off = co[0]
        co[0] += (n * ESZ[dt] + 3) // 4 * 4
        assert co[0] <= OFF_ADA
        return P.view(off, shape, dt)

    SMALL = calloc([560], F32)
    PADV = SMALL[:, 0:16]
    CORR = SMALL[:, 16:272].rearrange("p (a b c d) -> p a b c d", a=2, b=2, c=8)
    GVEC = SMALL[:, 272:320].rearrange("p (a b) -> p a b", a=6)
    ADAB = SMALL[:, 320:416].rearrange("p (a b) -> p a b", a=2)
    SINK = SMALL[:, 416:432]
    XPAD = SMALL[:, 432:560].rearrange("p (a b) -> p a b", a=8)
    MOD = calloc([2, 48, 2], F32)
    AV = calloc([2, 2, 2, 8], F32)
    ONESM = calloc([128], BF16)
    ONES1 = calloc([128], BF16)
    CTF = calloc([16], F32)
    CTS = calloc([NCH, 2], BF16)
    ESK = calloc([16], F32)
    EPSV = calloc([1], F32)
    CORR1 = calloc([1], F32)

    cload = P.dma("sp", SMALL, d_small, slot="small")
    P.dma("sp", CTF, d_cT, slot="ct")
    P.memset(ONESM, 1.0 / 1024.0)
    P.memset(ONES1, 1.0)
    P.memset(EPSV, EPS)

    b0, b1 = TILES[0]
    P.dma("sp", X[:, :, b0 * 128:b1 * 128], d_xT[:, :, b0 * 128:b1 * 128])

    def late_x_loads(dep):
        for (b0, b1) in TILES[1:]:
            P.S.op("sp", (lambda b0=b0, b1=b1: (lambda e: e.dma_start(out=X[:, :, b0 * 128:b1 * 128], in_=d_xT[:, :, b0 * 128:b1 * 128])))(),
                   reads=[], writes=[X[:, :, b0 * 128:b1 * 128]], chan="dma:x%d" % b0, inc=16, extra_deps=dep)
        P.S.op("sp", lambda e: e.dma_start(out=CX, in_=d_ctxT), reads=[], writes=[CX], chan="dma:cx", inc=16, extra_deps=dep)

    P.act(CTS.rearrange("p a b -> p (a b)"), CTF, AF.Silu)

    hslot = [0]

    def Hview(N):
        k = hslot[0]
        hslot[0] ^= 1
        return P.view(OFF_H + k * 8192, [NCH, N], BF16)

    tslot = [0]

    def TMPv(N):
        k = tslot[0]
        tslot[0] ^= 1
        return P.view(OFF_TMP + k * 2048, [N], F32)

    rslot = [0]

    def RSv(N):
        k = rslot[0]
        rslot[0] ^= 1
        return P.view(OFF_RSTD + k * 2048, [N], F32)

    adaslot = [0]

    bgq = []
    if do0:
        bgq += [(0, j) for j in range(24)]
    if do1:
        bgq += [(1, j) for j in range(24)]
    bgs = {"dma": 0, "cmp": 0}
    mb = P.psum[7]

    def bg_slot(k):
        if k < 8:
            return OFF_SCR + k * 4096, "adas%d" % k
        return OFF_ADA + (k % 2) * 4096, "ada%d" % (k % 2)

    def bg_dma(k):
        l, j = bgq[k]
        off_, nm_ = bg_slot(k)
        slot2 = P.view(off_, [2048], BF16)
        P.dma("pool", slot2, d_ada[l, j], slot=nm_)

    def bg_cmp(k):
        l, j = bgq[k]
        slot3 = P.view(bg_slot(k)[0], [NCH, 256], BF16)
        for mi in range(2):
            m = 2 * j + mi
            for kc in range(NCH):
                P.mm(mb[:, 2 * m:2 * m + 2], slot3[:, kc, mi * 128:(mi + 1) * 128], CTS[:, kc, :],
                     start=(kc == 0), stop=(kc == NCH - 1))
        m0 = 2 * j
        op_ = P.tt(MOD[:, l, m0:m0 + 2, :], mb[:, 2 * m0:2 * m0 + 4].rearrange("p (a b) -> p a b", a=2),
                   ADAB[:, l, m0:m0 + 2].unsqueeze(2).to_broadcast([128, 2, 2]), ALU.add)
        if k == 3:
            late_x_loads([op_])
        for n in range(2):
            if j == (1 + 3 * n) * 4 + 3:
                for col in range(2):
                    P.stt(AV[:, l, n, col, :], MOD[:, l, (1 + 3 * n) * 8:(2 + 3 * n) * 8, col], 1.0,
                          GVEC[:, 2 * n + l, :], ALU.add, ALU.mult)

    def bg_step(n=1):
        for _ in range(n):
            k = bgs["cmp"]
            if k >= len(bgq):
                return
            nxt = k + 1
            if nxt >= 8 and nxt < len(bgq) and bgs["dma"] == nxt:
                bg_dma(nxt)
                bgs["dma"] += 1
            while bgs["dma"] <= k:
                bg_dma(bgs["dma"])
                bgs["dma"] += 1
            bg_cmp(k)
            bgs["cmp"] += 1

    def bg_ensure(l, which):
        tgt = bgq.index((l, which * 4 + 3)) + 1
        while bgs["cmp"] < tgt:
            bg_step()

    for k_ in range(8):
        bg_dma(k_)
    bgs["dma"] = 8

    def norm_sq(xsrc, N):
        sq = P.view(OFF_SQ, [NCH, N], BF16)
        P.act(sq, xsrc, AF.Square)
        return sq

    def norm_stat(sq, N, rs=None, bank=None):
        bk = bank if bank is not None else P.bank()
        for c in range(NCH):
            P.mm(bk[:, 0:N], ONESM, sq[:, c, :], start=(c == 0), stop=(c == NCH - 1))
        if rs is None:
            rs = RSv(N)
        P.act(rs, bk[:, 0:N], AF.Ln, bias=EPSV)
        P.act(rs, rs, AF.Exp, scale=-0.5)
        return rs

    def norm_apply(xsrc, N, rs, l, n, col, hdst, final=False):
        for c in range(NCH):
            tmp = TMPv(N)
            P.tt(tmp, xsrc[:, c, :], rs, ALU.mult)
            if final:
                P.act(hdst[:, c, :], tmp, AF.Identity, scale=GVEC[:, 4, c:c + 1])
            else:
                P.act(hdst[:, c, :], tmp, AF.Identity, scale=AV[:, l, n, col, c:c + 1],
                      bias=MOD[:, l, (3 * n) * 8 + c, col:col + 1])

    def norm_mod(xsrc, N, l, n, col, hdst, final=False):
        sq = norm_sq(xsrc, N)
        rs = norm_stat(sq, N)
        norm_apply(xsrc, N, rs, l, n, col, hdst, final)

    def norm_pieces(xsrc, N, l, n, col, hdst):
        stt_ = {}

        def p_sq():
            stt_["sq"] = norm_sq(xsrc, N)

        def p_st():
            stt_["rs"] = norm_stat(stt_["sq"], N)

        def p_ap(c):
            def f():
                tmp = TMPv(N)
                P.tt(tmp, xsrc[:, c, :], stt_["rs"], ALU.mult)
                P.act(hdst[:, c, :], tmp, AF.Identity, scale=AV[:, l, n, col, c:c + 1],
                      bias=MOD[:, l, (3 * n) * 8 + c, col:col + 1])
            return f
        return [p_sq, p_st] + [p_ap(c) for c in range(NCH)]

    def interleave(main, side, every=1, lead=0):
        side = list(side)
        for _ in range(min(lead, len(side))):
            side.pop(0)()
        for i_, m_ in enumerate(main):
            m_()
            if side and (i_ % every) == every - 1:
                side.pop(0)()
        while side:
            side.pop(0)()

    def mlp_prep(group):
        GT = sum(N for _, N, _ in group)
        assert GT <= 896
        offs = []
        o = 0
        for xv, N, col in group:
            offs.append(o)
            o += N
        return dict(group=group, GT=GT, offs=offs,
                    H2=P.view(OFF_H, [NCH, GT], BF16), HID=P.view(OFF_SCR, [NHC, GT], BF16))

    def mlp_norm(l, st):
        for (xv, N, col), o in zip(st["group"], st["offs"]):
            norm_mod(xv, N, l, 1, col, st["H2"][:, :, o:o + N])

    def mlp_1(l, st, hooks=None):
        o_w1 = OFF_SCR + NHC * 896 * 2
        H2, HID = st["H2"], st["HID"]
        for j in range(16):
            k = j % 3
            w2d = P.view(o_w1 + k * 4096, [2048], BF16)
            w3d = P.view(o_w1 + k * 4096, [NCH, 256], BF16)
            P.dma("pool", w2d, d_w1[l, j], slot="w1_%d" % k)
            for hh in range(2):
                hc = 2 * j + hh
                for (xv, N, col), o in zip(st["group"], st["offs"]):
                    bk = P.bank()
                    for kc in range(NCH):
                        P.mm(bk[:, 0:N], w3d[:, kc, hh * 128:(hh + 1) * 128], H2[:, kc, o:o + N],
                             start=(kc == 0), stop=(kc == NCH - 1))
                    tmp = TMPv(N)
                    P.act(tmp, bk[:, 0:N], AF.Relu)
                    P.tt(HID[:, hc, o:o + N], tmp, tmp, ALU.mult)
            if hooks and j in hooks:
                for f_ in hooks[j]:
                    f_()

    def mlp_2(l, st):
        o_w2 = OFF_SCR + NHC * 896 * 2 + 3 * 4096
        assert o_w2 + 2 * 8192 <= ARENA_BYTES
        HID = st["HID"]
        for fo in range(NCH):
            k = fo % 2
            w2d = P.view(o_w2 + k * 8192, [4096], BF16)
            w3d = P.view(o_w2 + k * 8192, [NHC, 128], BF16)
            P.dma("pool", w2d, d_w2[l, fo], slot="w2_%d" % k)
            for (xv, N, col), o in zip(st["group"], st["offs"]):
                bk = P.bank()
                for hc in range(NHC):
                    P.mm(bk[:, 0:N], w3d[:, hc, :], HID[:, hc, o:o + N], start=(hc == 0), stop=(hc == NHC - 1))
                P.stt(xv[:, fo, :], bk[:, 0:N], MOD[:, l, 40 + fo, col:col + 1], xv[:, fo, :], ALU.mult, ALU.add)
            bg_step()

    def staged_norm_hooks(l, st_next, hooks, first_j=1, bank=None):
        applies = []
        pend = {}
        tiles = list(zip(st_next["group"], st_next["offs"]))
        for ti_, ((xv, N, col), o) in enumerate(tiles):
            j_sq = first_j + 4 * ti_
            j_st = first_j + 4 * ti_ + 4

            def f_sq(xv=xv, N=N, ti_=ti_):
                pend[ti_] = norm_sq(xv, N)

            def f_st(N=N, ti_=ti_):
                pend[ti_] = norm_stat(pend[ti_], N, bank=bank)

            hooks.setdefault(j_sq, []).append(f_sq)
            hooks.setdefault(j_st, []).insert(0, f_st)

            def f_ap(xv=xv, N=N, col=col, o=o, ti_=ti_):
                norm_apply(xv, N, pend[ti_], l, 1, col, st_next["H2"][:, :, o:o + N])

            applies.append(f_ap)
        return applies

    def mlp_layer(l, groups, final_fn=None, first_staged=None):
        sts = [mlp_prep(g) for g in groups]
        if first_staged is None:
            mlp_norm(l, sts[0])
        else:
            for f_ in first_staged:
                f_()
        for gi, st in enumerate(sts):
            hooks = {}
            applies = []
            if gi + 1 < len(sts):
                applies = staged_norm_hooks(l, sts[gi + 1], hooks)
            if final_fn is not None and gi > 0:
                final_fn(gi - 1, hooks)
            mlp_1(l, st, hooks)
            for f_ in applies:
                f_()
            mlp_2(l, st)
        if final_fn is not None:
            final_fn(len(sts) - 1, None)

    lat_tiles = [(X[:, :, b0 * 128:b1 * 128], (b1 - b0) * 128, 0) for (b0, b1) in TILES]

    if do0:
        bg_ensure(0, 1)
        o = OFF_SCR
        UL = P.view(o, [NCH, NT + 16], BF16); o += NCH * (NT + 16) * 2
        UC = P.view(o, [NCH, NCTX + 16], BF16); o += NCH * (NCTX + 16) * 2
        o_win = o
        WINU = [P.view(o + u_ * 4096, [NCH, 256], BF16) for u_ in range(4)]
        WINU2 = [P.view(o + u_ * 4096, [2048], BF16) for u_ in range(4)]
        DTS = [P.view(o, [NCH, 512], BF16), P.view(OFF_H, [NCH, 512], BF16)]
        Y1S = [P.view(o + 8192, [NCH, 512], BF16), P.view(OFF_H + 8192, [NCH, 512], BF16)]
        o += 16384
        WOUT2 = P.view(o, [8192], BF16); WOUT = P.view(o, [NCH, 1024], BF16); o += 16384
        WG2 = P.view(o, [2048], BF16); WG = P.view(o, [4, 2, 256], BF16); o += 4096
        WGS = P.view(o, [4, 2, 256], BF16); o += 4096
        PT_ = [P.view(o + i * 2112, [2, 528], BF16) for i in range(3)]; o += 3 * 2112
        PTP_ = [P.view(o + i * 2112, [2, 528], BF16) for i in range(1)]; o += 1 * 2112
        assert o <= ARENA_BYTES, o
        for u_ in range(4):
            P.dma("pool", WINU2[u_], d_win[:, u_ * 2048:(u_ + 1) * 2048], slot="win%d" % u_)
        P.memset(UC[:, :, 0:8], 0.0)
        P.memset(UC[:, :, NCTX + 8:NCTX + 16], 0.0)

        ph1 = [(xv, N, col, UL, 8 + b0 * 128) for (xv, N, col), (b0, b1) in zip(lat_tiles, TILES)]
        ph1.insert(1, (XPAD, 16, 0, UL, None))
        ph1.append((CX, NCTX, 1, UC, 8))
        hs = {}

        def p1_norm_pieces(i):
            xv, N, col, UB, ucol = ph1[i]
            hs[i] = Hview(N)
            return norm_pieces(xv, N, 0, 0, col, hs[i])

        def p1_proj_pieces(i):
            xv, N, col, UB, ucol = ph1[i]
            h = hs[i]

            def piece(cc):
                def f():
                    bk = P.bank()
                    for kc in range(NCH):
                        P.mm(bk[:, 0:N], WINU[cc // 2][:, kc, (cc % 2) * 128:(cc % 2) * 128 + 128], h[:, kc, :], start=(kc == 0), stop=(kc == NCH - 1))
                    if ucol is None:
                        P.tt(UB[:, cc, 0:8], bk[:, 0:8], PADV[:, 0:8], ALU.mult)
                        P.tt(UB[:, cc, NT + 8:NT + 16], bk[:, 8:16], PADV[:, 8:16], ALU.mult)
                    else:
                        P.copy(UB[:, cc, ucol:ucol + N], bk[:, 0:N])
                return f
            return [piece(cc) for cc in range(NCH)]

        for f_ in p1_norm_pieces(0):
            f_()
        for i in range(len(ph1)):
            side = p1_norm_pieces(i + 1) if i + 1 < len(ph1) else []
            interleave(p1_proj_pieces(i), side, lead=2)
            bg_step()
            if i == 4:
                P.dma("pool", WG2, d_wgrp, slot="wg")
                P.dma("pool", WOUT2, d_wout, slot="wout")
                for g_, w_ in enumerate((2, 4, 8, 16)):
                    P.ts(WGS[:, g_], WG[:, g_], 1.0 / w_, None, ALU.mult)
                P.ts(WG2, WG2, -1.0, None, ALU.mult)
        bg_ensure(0, 2)

        ph2 = [(xv, N, col, UL, 8 + b0 * 128, 0, b0 == 0, b1 == NBLK) for (xv, N, col), (b0, b1) in zip(lat_tiles, TILES)]
        ph2.append((CX, NCTX, 1, UC, 8, 1, True, True))

        def p2_pool_g(i, g, w):
            xv, N, col, UB, base, ci, first, last = ph2[i]
            DT = DTS[i % 2]
            if True:
                cs = slice(2 * g, 2 * g + 2)
                en = "pool" if g < 2 else "dve"
                if g < 2:
                    A_ = PTP_[0]
                    B_ = C_ = None
                else:
                    A_, B_, C_ = PT_
                dst = DT[:, cs, 0:N]

                def u(a, n):
                    return UB[:, cs, base + a:base + a + n]
                if w == 2:
                    P.tt(dst, u(-1, N), u(0, N), ALU.add, eng=en)
                elif w == 4:
                    P.tt(A_[:, :, 0:N + 2], u(-2, N + 2), u(-1, N + 2), ALU.add, eng=en)
                    P.tt(dst, A_[:, :, 0:N], A_[:, :, 2:N + 2], ALU.add, eng=en)
                elif w == 8:
                    P.tt(A_[:, :, 0:N + 6], u(-4, N + 6), u(-3, N + 6), ALU.add, eng=en)
                    P.tt(B_[:, :, 0:N + 4], A_[:, :, 0:N + 4], A_[:, :, 2:N + 6], ALU.add, eng=en)
                    P.tt(dst, B_[:, :, 0:N], B_[:, :, 4:N + 4], ALU.add, eng=en)
                else:
                    P.tt(A_[:, :, 0:N + 14], u(-8, N + 14), u(-7, N + 14), ALU.add, eng=en)
                    P.tt(B_[:, :, 0:N + 12], A_[:, :, 0:N + 12], A_[:, :, 2:N + 14], ALU.add, eng=en)
                    P.tt(C_[:, :, 0:N + 8], B_[:, :, 0:N + 8], B_[:, :, 4:N + 12], ALU.add, eng=en)
                    P.tt(dst, C_[:, :, 0:N], C_[:, :, 8:N + 8], ALU.add, eng=en)
                if first:
                    P.tt(DT[:, cs, 0:8], DT[:, cs, 0:8], CORR[:, ci, 0, cs, :], ALU.mult, eng=en)
                if last:
                    P.tt(DT[:, cs, N - 8:N], DT[:, cs, N - 8:N], CORR[:, ci, 1, cs, :], ALU.mult, eng=en)

        def p2_pool_pieces(i):
            return [(lambda g=g, w=w: p2_pool_g(i, g, w)) for g, w in enumerate((2, 4, 8, 16))]

        def p2_mm_pieces(i):
            xv, N, col, UB, base, ci, first, last = ph2[i]
            DT = DTS[i % 2]
            Y1 = Y1S[i % 2]

            def grp(cc):
                def f():
                    g = cc // 2
                    bk = P.bank()
                    cw = slice((cc % 2) * 128, (cc % 2) * 128 + 128)
                    for k2 in range(2):
                        P.mm(bk[:, 0:N], WGS[:, g, k2, cw], DT[:, 2 * g + k2, 0:N], start=(k2 == 0), stop=False)
                    for k2 in range(2):
                        P.mm(bk[:, 0:N], WG[:, g, k2, cw], UB[:, 2 * g + k2, base:base + N], start=False, stop=(k2 == 1))
                    P.act(Y1[:, cc, 0:N], bk[:, 0:N], AF.Copy, scale=GVEC[:, 5, cc:cc + 1])
                return f

            def wout(cc):
                def f():
                    bk = P.bank()
                    for kc in range(NCH):
                        P.mm(bk[:, 0:N], WOUT[:, kc, cc * 128:(cc + 1) * 128], Y1[:, kc, 0:N], start=(kc == 0), stop=(kc == NCH - 1))
                    P.stt(xv[:, cc, :], bk[:, 0:N], MOD[:, 0, 16 + cc, col:col + 1], xv[:, cc, :], ALU.mult, ALU.add)
                return f
            return [grp(cc) for cc in range(NCH)] + [wout(cc) for cc in range(NCH)]

        ctx_tile = (CX, NCTX, 1)
        l0_groups = [[lat_tiles[0], lat_tiles[1]], [lat_tiles[2], lat_tiles[3]], [lat_tiles[4], ctx_tile]]
        st0 = mlp_prep(l0_groups[0])
        hk0 = {}
        first_applies = staged_norm_hooks(0, st0, hk0, first_j=3)
        for f_ in p2_pool_pieces(0):
            f_()
        stage_seq = []
        for j_ in sorted(hk0):
            stage_seq += hk0[j_]
        for i in range(len(ph2)):
            side = p2_pool_pieces(i + 1) if i + 1 < len(ph2) else []
            main = p2_mm_pieces(i)
            interleave(main[:8], side[:2], every=4)
            bg_step()
            interleave(main[8:], side[2:], every=4)
            bg_step()
            if i >= 2 and stage_seq:
                stage_seq.pop(0)()

        while stage_seq:
            stage_seq.pop(0)()
        bg_ensure(0, 5)
        mlp_layer(0, l0_groups, first_staged=first_applies)
        if do1:
            bg_ensure(1, 5)
    elif do1:
        bg_ensure(1, 5)

    if do1:
        o = OFF_ADA
        QT = P.view(o, [NCH, NT], BF16); o += NCH * NT * 2
        KTK = [P.view(o + i * (NT + NCTX) * 2, [NT + NCTX], BF16) for i in range(2)]; o += 2 * (NT + NCTX) * 2
        VA = [P.view(o + i * (NBLK + 2) * 256, [NBLK + 2, 128], BF16) for i in range(2)]; o += 2 * (NBLK + 2) * 256
        o_ph = o
        ROPE2 = P.view(o, [2 * NT], BF16); COS = P.view(o, [NT], BF16); SIN = P.view(o + NT * 2, [NT], BF16); o += 4 * NT
        WV = P.view(o, [NCH, 128], BF16); WV2 = P.view(o, [1024], BF16)
        WQC = [P.view(o + 2048 + c * 2048, [NCH, 128], BF16) for c in range(9)]
        WQC2 = [P.view(o + 2048 + c * 2048, [1024], BF16) for c in range(9)]
        o += 10 * 2048
        PERM = P.view(o, [128], BF16); o += 256
        QB = [P.view(o + i * 1024, [512], BF16) for i in range(2)]; o += 2048
        RTS = [P.view(o + i * 2048, [512], F32) for i in range(4)]; o += 8192
        rtc = [0]
        assert o <= ARENA_BYTES, o
        P.dma("pool", WQC2[0], d_wqkv[:, 1024:2048], slot="wq0")
        P.dma("pool", PERM, d_mask[:, 256:384], slot="perm")
        P.dma("pool", ROPE2, d_rope, slot="rope")
        for c in range(1, 9):
            P.dma("pool", WQC2[c], d_wqkv[:, 1024 + c * 1024:1024 + (c + 1) * 1024], slot="wq%d" % c)
        P.dma("pool", WV2, d_wqkv[:, 0:1024], slot="wv")
        P.memset(KTK[0][64:128, :], 0.0)
        P.memset(VA[0][:, :, 64:128], 1.0)
        P.memset(VA[1][:, :, 0:64], 1.0)
        P.memset(KTK[1][0:64, :], 0.0)
        P.act(ESK, SINK, AF.Exp)

        l1t = [(xv, N, 0, b0, b1) for (xv, N, col), (b0, b1) in zip(lat_tiles, TILES)] + [(CX, NCTX, 1, NBLK, NBLK + 2)]
        hs1 = {}

        def q_norm_pieces(i):
            xv, N, col, b0, b1 = l1t[i]
            hs1[i] = Hview(N)
            return norm_pieces(xv, N, 1, 0, col, hs1[i])

        def q_proj_pieces(i):
            xv, N, col, b0, b1 = l1t[i]
            h = hs1[i]
            t0 = b0 * 128
            pcs = []
            pend = {}

            def rope_fin(c, ba, qb_):
                bb = P.bank()
                P.mm(bb[:, 0:N], PERM, qb_[:, 0:N])
                t1 = RTS[rtc[0] % 4][:, 0:N]
                t2 = RTS[(rtc[0] + 1) % 4][:, 0:N]
                rtc[0] += 2
                P.tt(t1, ba[:, 0:N], COS[:, t0:t0 + N], ALU.mult)
                P.tt(t2, bb[:, 0:N], SIN[:, t0:t0 + N], ALU.mult)
                if c < NCH:
                    P.tt(QT[:, c, t0:t0 + N], t1, t2, ALU.add, eng="pool")
                else:
                    P.tt(KTK[0][0:64, t0:t0 + N], t1[0:64], t2[0:64], ALU.add, eng="pool")
                    P.tt(KTK[1][64:128, t0:t0 + N], t1[64:128], t2[64:128], ALU.add, eng="pool")

            def vpiece():
                nb = b1 - b0
                bv = P.bank()
                for bi in range(nb):
                    for kc in range(NCH):
                        P.mm(bv[:, bi * 128:(bi + 1) * 128], h[:, kc, bi * 128:(bi + 1) * 128], WV[:, kc, :],
                             start=(kc == 0), stop=(kc == NCH - 1))
                bv3 = bv[:, 0:nb * 128].rearrange("p (a b) -> p a b", a=nb)
                P.act(VA[0][:, b0:b1, 0:64], bv3[:, :, 0:64], AF.Copy)
                P.act(VA[1][:, b0:b1, 64:128], bv3[:, :, 64:128], AF.Copy)

            if col == 0:
                def qpiece(c):
                    def f():
                        ba = P.bank()
                        for kc in range(NCH):
                            P.mm(ba[:, 0:N], WQC[c][:, kc, :], h[:, kc, :], start=(kc == 0), stop=(kc == NCH - 1))
                        qb_ = QB[c % 2]
                        P.act(qb_[:, 0:N], ba[:, 0:N], AF.Copy)
                        if "p" in pend:
                            rope_fin(*pend["p"])
                        pend["p"] = (c, ba, qb_)
                    return f
                pcs = [qpiece(c) for c in range(NCH + 1)]

                def last():
                    rope_fin(*pend["p"])
                    vpiece()
                pcs.append(last)
            else:
                def kpiece():
                    ba = P.bank()
                    for kc in range(NCH):
                        P.mm(ba[:, 0:N], WQC[8][:, kc, :], h[:, kc, :], start=(kc == 0), stop=(kc == NCH - 1))
                    P.act(KTK[0][0:64, NT:NT + NCTX], ba[0:64, 0:N], AF.Copy)
                    P.act(KTK[1][64:128, NT:NT + NCTX], ba[64:128, 0:N], AF.Copy)
                pcs = [kpiece, vpiece]
            return pcs

        for f_ in q_norm_pieces(0):
            f_()
        for i in range(len(l1t)):
            side = q_norm_pieces(i + 1) if i + 1 < len(l1t) else []
            interleave(q_proj_pieces(i), side)

        o = o_ph
        WO2 = P.view(o, [8192], BF16); WO = P.view(o, [NCH, 1024], BF16); o += 16384
        PTS = [P.view(o + i * 1024, [4, 128], BF16) for i in range(10)]; o += 10 * 1024
        OTS = [P.view(OFF_H + i * 8192, [NCH, 512], BF16) for i in range(2)]
        DNS = [P.view(o + i * 2048, [4, 128], F32) for i in range(3)]; o += 3 * 2048
        RC2 = [P.view(o + i * 2048, [4, 128], F32) for i in range(3)]; o += 3 * 2048
        MSK2 = P.view(o, [256], BF16); MSK = P.view(o, [2, 128], BF16); o += 512
        assert o <= ARENA_BYTES
        P.dma("pool", MSK2, d_mask[:, 0:256], slot="mask")
        P.dma("pool", WO2, d_wo, slot="wo")
        SBK = [P.psum[i] for i in range(3)]
        OBK = [P.psum[4 + i] for i in range(4)]
        sctr = [0]

        units = [(qb, kvh, half) for qb in range(NBLK) for half in range(2) for kvh in range(2)]
        tile_of = {}
        for ti, (b0, b1) in enumerate(TILES):
            for qb in range(b0, b1):
                tile_of[qb] = ti

        def kbs_of(qb):
            kbs = []
            if qb > 0:
                kbs.append((qb - 1, 0))
            kbs.append((qb, None))
            if qb < NBLK - 1:
                kbs.append((qb + 1, 1))
            kbs.append((NBLK, None))
            kbs.append((NBLK + 1, None))
            return kbs

        def emit_S(ui, i):
            qb, kvh, half = units[ui]
            c0 = half * 4
            kb, mk = kbs_of(qb)[i]
            sb = SBK[sctr[0] % 3]
            sctr[0] += 1
            P.mm(sb[:, :], KTK[kvh][:, kb * 128:(kb + 1) * 128], QT[:, c0:c0 + 4, qb * 128:(qb + 1) * 128])
            pt = PTS[(ui % 2) * 5 + i]
            P.act(pt, sb[:, :].rearrange("p (a b) -> p a b", a=4), AF.Exp, scale=0.125)
            if mk is not None:
                P.tt(pt, pt, MSK[:, mk, :].unsqueeze(1).to_broadcast([128, 4, 128]), ALU.mult, eng="pool")

        def emit_PV(ui, i):
            qb, kvh, half = units[ui]
            kbs = kbs_of(qb)
            n = len(kbs)
            kb, mk = kbs[i]
            ob = OBK[ui % 4]
            pt = PTS[(ui % 2) * 5 + i].rearrange("p a b -> p (a b)")
            P.mm(ob[:, :], VA[kvh][:, kb, :], pt, start=(i == 0), stop=(i == n - 1))

        def emit_den(ui):
            qb, kvh, half = units[ui]
            c0 = half * 4
            drows = slice(64, 128) if kvh == 0 else slice(0, 64)
            ob = OBK[ui % 4]
            dn = DNS[(ui // 2) % 3]
            h0 = kvh * 8 + c0
            P.tt(dn[drows], ob[drows, :].rearrange("p (a b) -> p a b", a=4),
                 ESK[drows, h0:h0 + 4].unsqueeze(2).to_broadcast([64, 4, 128]), ALU.add)

        def emit_recip(pi):
            dn = DNS[pi % 3]
            r2 = RC2[pi % 3]
            P.act(dn, dn, AF.Ln)
            P.act(dn, dn, AF.Exp, scale=-1.0)
            P.dma("sp", r2[0:64], dn[64:128], slot="rsw%da" % (pi % 3))
            P.dma("sp", r2[64:128], dn[0:64], slot="rsw%db" % (pi % 3))

        def emit_mult(ui):
            qb, kvh, half = units[ui]
            rows = slice(kvh * 64, kvh * 64 + 64)
            c0 = half * 4
            ti = tile_of[qb]
            b0, b1 = TILES[ti]
            qoff = (qb - b0) * 128
            ob = OBK[ui % 4]
            r2 = RC2[(ui // 2) % 3]
            P.tt(OTS[ti % 2][rows, c0:c0 + 4, qoff:qoff + 128], ob[rows, :].rearrange("p (a b) -> p a b", a=4),
                 r2[rows], ALU.mult)

        deferred = []

        def wo_closure(ti, fo):
            def f():
                xv, N, col = lat_tiles[ti]
                bk = P.psum[3]
                for c in range(NCH):
                    P.mm(bk[:, 0:N], WO[:, c, fo * 128:(fo + 1) * 128], OTS[ti % 2][:, c, 0:N], start=(c == 0), stop=(c == NCH - 1))
                P.stt(xv[:, fo, :], bk[:, 0:N], MOD[:, 1, 16 + fo, 0:1], xv[:, fo, :], ALU.mult, ALU.add)
            return f

        def finish_pair(pi):
            ua, ub = 2 * pi, 2 * pi + 1
            emit_mult(ua)
            emit_mult(ub)
            qb = units[ub][0]
            ti = tile_of[qb]
            b0, b1 = TILES[ti]
            if qb == b1 - 1 and units[ub][2] == 1:
                for fo in range(NCH):
                    deferred.append(wo_closure(ti, fo))

        L1M = [(0, 3), (3, 6), (6, 9), (9, 12), (12, 15), (15, 17)]
        m_tiles = [(X[:, :, b0 * 128:b1 * 128], (b1 - b0) * 128, 0) for (b0, b1) in L1M]
        l1_groups = [[m_tiles[0], m_tiles[1]], [m_tiles[2], m_tiles[3]], [m_tiles[4], m_tiles[5]]]
        st1 = mlp_prep(l1_groups[0])
        att_hooks = {}
        l1_first_applies = staged_norm_hooks(1, st1, att_hooks, first_j=40, bank=P.psum[3])

        for i in range(len(kbs_of(units[0][0]))):
            emit_S(0, i)
        for ui, (qb, kvh, half) in enumerate(units):
            nP = len(kbs_of(qb))
            nS = len(kbs_of(units[ui + 1][0])) if ui + 1 < len(units) else 0
            for i in range(max(nP, nS)):
                if i < nS:
                    emit_S(ui + 1, i)
                if i < nP:
                    emit_PV(ui, i)
            emit_den(ui)
            if ui % 2 == 1:
                emit_recip(ui // 2)
            elif ui >= 2:
                finish_pair(ui // 2 - 1)
            if deferred:
                deferred.pop(0)()
            for f_ in att_hooks.get(ui, []):
                f_()
        finish_pair(len(units) // 2 - 1)
        while deferred:
            deferred.pop(0)()
        P.bank_rr = 0

        outs = []
        fin = {0: [0, 1], 1: [2, 3], 2: [4, 5]}
        FRS = [P.view(OFF_ADA + i * 2048, [512], F32) for i in range(2)]

        def final_fn(gi, hooks):
            tis = fin[gi]
            pend = {}
            for k_, ti in enumerate(tis):
                xv, N, col = m_tiles[ti]
                b0, b1 = L1M[ti]

                def f_sq(xv=xv, N=N, k_=k_):
                    pend[k_] = norm_sq(xv, N)

                def f_fin(xv=xv, N=N, k_=k_, b0=b0, b1=b1):
                    rs = norm_stat(pend[k_], N, rs=FRS[k_][:, 0:N])
                    norm_apply(xv, N, rs, 0, 0, 0, xv, final=True)
                    outs.append(P.dma("sp", d_out[:, :, b0 * 128:b1 * 128], xv))

                if hooks is None:
                    f_sq()
                    f_fin()
                else:
                    hooks.setdefault(10 + 3 * k_, []).append(f_sq)
                    hooks.setdefault(12 + 3 * k_, []).append(f_fin)

        mlp_layer(1, l1_groups, final_fn=final_fn, first_staged=l1_first_applies)
        S.op("sp", None, extra_deps=outs)
    else:
        outs = []
        for (xv, N, col), (b0, b1) in zip(lat_tiles, TILES):
            outs.append(P.dma("sp", d_x1[:, :, b0 * 128:b1 * 128], xv))
        outs.append(P.dma("sp", d_c1, CX))
        S.op("sp", None, extra_deps=outs)

    S.emit()
    return nc


POOL_WINDOWS = (2, 4, 8, 16)


def _fm(a):
    T = a.shape[0]
    return np.ascontiguousarray(a.reshape(T, NCH, 128).transpose(2, 1, 0))


def _wl(w):
    K, N = w.shape
    return np.ascontiguousarray(w.reshape(K // 128, 128, N).transpose(1, 0, 2).reshape(128, (K // 128) * N))


def _corr_tables(t0g, n_main, L):
    c = np.ones((2, NCH, 8), np.float32)
    for g, w in enumerate(POOL_WINDOWS):
        for side in range(2):
            for i in range(8):
                t = t0g + i if side == 0 else t0g + n_main - 8 + i
                lo = min(max(t - w // 2, 0), L)
                hi = min(max(t + w // 2, 0), L)
                c[side, 2 * g:2 * g + 2, i] = np.float32(w) / np.float32(hi - lo)
    return c


def _rope_tables(t0g):
    t = np.arange(t0g, t0g + NT)
    row = (t // 64).astype(np.float32)
    colp = (t % 64).astype(np.float32)
    inv = (np.float32(10000.0) ** (-np.arange(0, 32, 2, dtype=np.float32) / np.float32(32))).astype(np.float32)
    cos = np.zeros((128, NT), np.float32)
    sin = np.zeros((128, NT), np.float32)
    for p in range(128):
        d = p % 64
        pos = row if d < 32 else colp
        dd = d % 32
        ang = (pos * inv[dd % 16]).astype(np.float32)
        cos[p] = np.cos(ang)
        sin[p] = -np.sin(ang) if dd < 16 else np.sin(ang)
    return np.concatenate([cos, sin], axis=1)


def _shared_weights(inp):
    f = np.float32
    sh = {}
    ada_w = np.asarray(inp["ada_w"], f)
    sh["ada_r"] = np.ascontiguousarray(ada_w.reshape(2, NCH, 128, 24, 256).transpose(0, 3, 2, 1, 4).reshape(2, 24, 128, 2048))
    w1 = np.asarray(inp["mlp_w1"], f)
    sh["w1_r"] = np.ascontiguousarray(w1.reshape(2, NCH, 128, 16, 256).transpose(0, 3, 2, 1, 4).reshape(2, 16, 128, 2048))
    w2 = np.asarray(inp["mlp_w2"], f)
    sh["w2_r"] = np.ascontiguousarray(w2.reshape(2, NHC, 128, NCH, 128).transpose(0, 3, 2, 1, 4).reshape(2, NCH, 128, 4096))
    win = np.asarray(inp["pool_w_in"], f)[0]
    sh["win_r"] = np.concatenate([_wl(win[:, u * 256:(u + 1) * 256]) for u in range(4)], axis=1)
    sh["wout_r"] = _wl(np.asarray(inp["pool_w_out"], f)[0])
    wg = np.asarray(inp["pool_w_grp"], f)[0]
    sh["wgrp_r"] = np.ascontiguousarray(wg.reshape(4, 2, 128, 256).transpose(2, 0, 1, 3).reshape(128, 2048))
    wqkv = np.asarray(inp["attn_w_qkv"], f)[0]
    hd_order = []
    for c in range(8):
        hd_order += [c, 8 + c]
    qcols = np.concatenate([np.arange(h * 64, h * 64 + 64) for h in hd_order])
    partner = np.array([(d + 16) if (d % 32) < 16 else (d - 16) for d in range(64)])
    qpcols = np.concatenate([h * 64 + partner for h in hd_order])
    kcols = 1024 + np.arange(128)
    kpcols = 1024 + np.concatenate([h * 64 + partner for h in range(2)])
    vcols = 1152 + np.arange(128)
    units = [vcols]
    for c in range(8):
        units.append(qcols[c * 128:(c + 1) * 128])
    units.append(kcols)
    sh["wqkv_r"] = np.concatenate([_wl(wqkv[:, u]) for u in units], axis=1)
    wo = np.asarray(inp["attn_w_o"], f)[0]
    sh["wo_r"] = _wl(wo[qcols, :])
    ml = np.zeros((128, 2, 128), f)
    jj = np.arange(128)[:, None]
    ii = np.arange(128)[None, :]
    ml[:, 0, :] = (jj >= ii)
    ml[:, 1, :] = (jj <= ii)
    pm = np.zeros((128, 128), f)
    for m in range(128):
        pm[(m // 64) * 64 + partner[m % 64], m] = 1.0
    sh["masks"] = np.concatenate([ml.reshape(128, 256), pm], axis=1)
    return sh


def _core_inputs(inp, core):
    f = np.float32
    b, hf = core // 2, core % 2
    x = np.asarray(inp["x"], f)[b]
    t0g = 0 if hf == 0 else SEQ - NT
    d = {}
    d["xT"] = _fm(x[t0g:t0g + NT])
    d["ctxT"] = _fm(np.asarray(inp["ctx"], f)[b])
    cc = np.stack([np.asarray(inp["c"], f)[b], np.asarray(inp["c_ctx"], f)], axis=1)
    d["cT"] = np.ascontiguousarray(cc.reshape(NCH, 128, 2).transpose(1, 0, 2).reshape(128, 16))
    small = np.zeros((128, 560), f)
    xpad = np.zeros((16, D), f)
    if hf == 0:
        small[:, 8:16] = 1.0
        xpad[8:16] = x[NT:NT + 8]
    else:
        small[:, 0:8] = 1.0
        xpad[0:8] = x[t0g - 8:t0g]
    corr = np.stack([_corr_tables(t0g, NT, SEQ), _corr_tables(0, NCTX, NCTX)], axis=0)
    small[:, 16:272] = corr.reshape(1, 256)
    gv = np.stack([np.asarray(inp["norm_mix_g"], f)[0], np.asarray(inp["norm_mix_g"], f)[1],
                   np.asarray(inp["norm_mlp_g"], f)[0], np.asarray(inp["norm_mlp_g"], f)[1],
                   np.asarray(inp["final_g"], f), np.asarray(inp["pool_scale"], f)[0]], axis=0)
    small[:, 272:320] = gv.reshape(6, NCH, 128).transpose(2, 0, 1).reshape(128, 48)
    ab = np.asarray(inp["ada_b"], f)
    small[:, 320:416] = ab.reshape(2, 48, 128).transpose(2, 0, 1).reshape(128, 96)
    small[:, 416:432] = np.asarray(inp["attn_sink"], f)[0][None, :]
    small[:, 432:560] = _fm(xpad).reshape(128, 128)
    d["small"] = small
    d["rope"] = _rope_tables(t0g)
    return d


def _run(mode, shared, percore):
    nc = build(mode)
    names_by_mode = {
        "ALL": ["ada_r", "w1_r", "w2_r", "win_r", "wout_r", "wgrp_r", "wqkv_r", "wo_r", "masks"],
        "L0": ["ada_r", "w1_r", "w2_r", "win_r", "wout_r", "wgrp_r"],
        "L1": ["ada_r", "w1_r", "w2_r", "wqkv_r", "wo_r", "masks"],
    }[mode]
    in_maps = []
    for c in range(8):
        m = {k: shared[k] for k in names_by_mode}
        for k, v in percore[c].items():
            if k == "rope" and mode == "L0":
                continue
            m[k] = v
        in_maps.append(m)
    return run_bass_kernel_spmd(nc, in_maps, core_ids=list(range(8))).results


FUSED = True


def kernel(**inputs):
    shared = _shared_weights(inputs)
    percore = [_core_inputs(inputs, c) for c in range(8)]
    if FUSED:
        res = _run("ALL", shared, percore)
    else:
        r0 = _run("L0", shared, percore)
        for c in range(8):
            percore[c]["xT"] = r0[c]["x1T"]
            percore[c]["ctxT"] = r0[c]["c1T"]
        res = _run("L1", shared, percore)
    out = np.zeros((NB, SEQ, D), np.float32)
    for c in range(8):
        b, hf = c // 2, c % 2
        o = res[c]["outT"]
        tok = o.transpose(2, 1, 0).reshape(NT, D)
        if hf == 0:
            out[b, 0:2048] = tok[0:2048]
        else:
            out[b, 2048:4096] = tok[NT - 2048:NT]
    return out
```

```python
import numpy as np
from contextlib import ExitStack
import concourse.bass as bass
import concourse.mybir as mybir
from concourse.bass_utils import run_bass_kernel_spmd

F32 = mybir.dt.float32
BF16 = mybir.dt.bfloat16
AF = mybir.ActivationFunctionType
ALU = mybir.AluOpType
ESZ = {F32: 4, BF16: 2}

D = 1024
NCH = 8
SEQ = 4096
NB = 4
NT = 2176
NBLK = 17
NCTX = 256
DFF = 4096
NHC = 32
EPS = 1e-6
ARENA_BYTES = 212800
PSUM_BANK = 2048
STRICT_SAME_ENGINE = True


class _Op:
    __slots__ = ("id", "eng", "chan", "fn", "deps", "signal", "waits", "pos", "inc", "sigval")


class Sched:
    BK = 2048

    def __init__(self, nc, psum_names):
        self.nc = nc
        self.ops = []
        self.streams = {k: [] for k in ("pe", "act", "dve", "pool", "sp")}
        self.chan_ops = {}
        self.buckets = {}
        self.psum_names = psum_names

    def rects(self, ap):
        t = ap.tensor
        name = t.name
        if name in self.psum_names:
            b = self.psum_names[name]
            return "ps", [(0, 128, b * PSUM_BANK, (b + 1) * PSUM_BANK)]
        if name != "arena":
            return None, []
        esz = ESZ[ap.dtype]
        dims = [tuple(d) for d in ap.ap]
        pstride, pcount = dims[0]
        off = int(ap.offset)
        if pstride > 0:
            p0 = off // pstride
            foff = off % pstride
        else:
            p0 = 0
            foff = off
        p1 = p0 + pcount
        free = dims[1:]
        if not free:
            return "sb", [(p0, p1, foff * esz, (foff + 1) * esz)]
        ls, ln = free[-1]
        run = ((ln - 1) * abs(ls) + 1)
        outer = free[:-1]
        tot = 1
        for s, n in outer:
            tot *= n
        res = []
        if tot <= 64:
            idxs = [0]
            for s, n in outer:
                idxs = [b + s * i for b in idxs for i in range(n)]
            for b in idxs:
                st = foff + b
                res.append((p0, p1, st * esz, (st + run) * esz))
        else:
            mx = foff + sum(s * (n - 1) for s, n in outer)
            res.append((p0, p1, foff * esz, (mx + run) * esz))
        return "sb", res

    def _query_insert(self, op, accs):
        deps = {}
        newrecs = []
        for space, r, isw in accs:
            p0, p1, b0, b1 = r
            seen = set()
            for bk in range(b0 // self.BK, (b1 - 1) // self.BK + 1):
                lst = self.buckets.get((space, bk))
                if not lst:
                    continue
                for rec in lst:
                    if not rec[7] or id(rec) in seen:
                        continue
                    seen.add(id(rec))
                    if rec[2] < b1 and b0 < rec[3] and rec[0] < p1 and p0 < rec[1]:
                        if rec[5] or isw:
                            kind = "RAW" if (rec[5] and not isw) else "W"
                            prev = deps.get(rec[4])
                            if prev is None or kind == "RAW":
                                deps[rec[4]] = kind
                        if isw and p0 <= rec[0] and rec[1] <= p1 and b0 <= rec[2] and rec[3] <= b1:
                            rec[7] = False
                        elif (not isw) and (not rec[5]) and rec[6] == op.chan and rec[0] == p0 and rec[1] == p1 \
                                and rec[2] == b0 and rec[3] == b1:
                            rec[7] = False
            newrecs.append((space, [p0, p1, b0, b1, op.id, isw, op.chan, True]))
        for space, rec in newrecs:
            for bk in range(rec[2] // self.BK, (rec[3] - 1) // self.BK + 1):
                lst = self.buckets.setdefault((space, bk), [])
                lst.append(rec)
                if len(lst) > 96:
                    lst[:] = [x for x in lst if x[7]]
        deps.pop(op.id, None)
        return deps

    def op(self, eng, fn, reads=(), writes=(), chan=None, inc=1, extra_deps=()):
        o = _Op()
        o.id = len(self.ops)
        o.eng = eng
        o.chan = chan or eng
        o.fn = fn
        o.inc = inc
        o.signal = chan is not None
        o.waits = []
        o.sigval = 0
        accs = []
        for ap in reads:
            if ap is None or isinstance(ap, (int, float)):
                continue
            sp, rs = self.rects(ap)
            for r in rs:
                accs.append((sp, r, sp == "ps"))
        for ap in writes:
            sp, rs = self.rects(ap)
            for r in rs:
                accs.append((sp, r, True))
        o.deps = self._query_insert(o, accs)
        for d in extra_deps:
            o.deps[d.id] = "RAW"
        self.ops.append(o)
        self.streams[eng].append(o)
        lst = self.chan_ops.setdefault(o.chan, [])
        lst.append(o)
        o.pos = len(lst)
        return o

    def plan(self):
        known = {e: {} for e in self.streams}
        snap = {}
        for o in self.ops:
            K = known[o.eng]
            need = {}
            for d, kind in o.deps.items():
                dop = self.ops[d]
                if dop.chan == o.chan and dop.chan == o.eng:
                    if o.eng == "pe" or (kind != "RAW" and not STRICT_SAME_ENGINE):
                        continue
                if need.get(dop.chan, 0) < dop.pos:
                    need[dop.chan] = dop.pos
            waits = []
            for chan, pos in need.items():
                if K.get(chan, 0) >= pos:
                    continue
                waits.append((chan, pos))
            for chan, pos in waits:
                tgt = self.chan_ops[chan][pos - 1]
                tgt.signal = True
                if K.get(chan, 0) < pos:
                    K[chan] = pos
                for c2, p2 in snap[tgt.id].items():
                    if K.get(c2, 0) < p2:
                        K[c2] = p2
            o.waits = waits
            s = dict(K)
            s[o.chan] = o.pos
            snap[o.id] = s
        for chan, lst in self.chan_ops.items():
            cnt = 0
            for o in lst:
                if o.signal:
                    cnt += o.inc
                o.sigval = cnt

    def emit(self):
        nc = self.nc
        self.plan()
        with ExitStack() as es:
            sems = {}
            for chan in self.chan_ops:
                if any(o.signal for o in self.chan_ops[chan]):
                    sems[chan] = es.enter_context(nc.semaphore("s_" + chan.replace(":", "_")))
            block = es.enter_context(nc.Block())

            def run(stream):
                def f(e):
                    for o in self.streams[stream]:
                        for chan, pos in o.waits:
                            e.wait_ge(sems[chan], self.chan_ops[chan][pos - 1].sigval)
                        ins = o.fn(e) if o.fn is not None else None
                        if o.signal and ins is not None:
                            ins.then_inc(sems[o.chan], o.inc)
                return f

            block.tensor(run("pe"))
            block.scalar(run("act"))
            block.vector(run("dve"))
            block.gpsimd(run("pool"))
            block.sync(run("sp"))


class Prog:
    def __init__(self, nc):
        self.nc = nc
        self.psum = []
        names = {}
        for b in range(8):
            t = nc.alloc_psum_tensor("psb%d" % b, [128, 512], F32)
            names["psb%d" % b] = b
            self.psum.append(t)
        self.S = Sched(nc, names)
        self.arena = nc.alloc_sbuf_tensor("arena", [128, ARENA_BYTES // 2], BF16)
        self.bank_rr = 0
        self.dma_n = 0

    def view(self, off, shape, dt):
        n = 1
        for s in shape:
            n *= s
        nb = n * ESZ[dt]
        assert off % 4 == 0 and off + nb <= ARENA_BYTES, (off, nb)
        ap = self.arena[:, off // 2:(off + nb) // 2]
        if dt != BF16:
            ap = ap.bitcast(dt)
        if len(shape) > 1:
            names = "abcdef"[:len(shape)]
            pat = "p (%s) -> p %s" % (" ".join(names), " ".join(names))
            ap = ap.rearrange(pat, **{names[i]: shape[i] for i in range(len(shape) - 1)})
        return ap

    def bank(self):
        b = self.bank_rr
        self.bank_rr = (b + 1) % 7
        return self.psum[b]

    def mm(self, out, lhsT, rhs, start=True, stop=True):
        return self.S.op("pe", lambda e: e.matmul(out, lhsT, rhs, start=start, stop=stop),
                         reads=[lhsT, rhs], writes=[out])

    def act(self, out, in_, func, scale=1.0, bias=0.0, eng="act"):
        rd = [in_]
        if not isinstance(scale, (int, float)):
            rd.append(scale)
        if not isinstance(bias, (int, float)):
            rd.append(bias)
        return self.S.op(eng, lambda e: e.activation(out, in_, func, bias=bias, scale=scale), reads=rd, writes=[out])

    def tt(self, out, in0, in1, op, eng="dve"):
        return self.S.op(eng, lambda e: e.tensor_tensor(out, in0, in1, op), reads=[in0, in1], writes=[out])

    def ts(self, out, in0, s1, s2, op0, op1=None, eng="dve"):
        rd = [in0]
        for s in (s1, s2):
            if s is not None and not isinstance(s, (int, float)):
                rd.append(s)
        if op1 is None:
            return self.S.op(eng, lambda e: e.tensor_scalar(out, in0, s1, None, op0), reads=rd, writes=[out])
        return self.S.op(eng, lambda e: e.tensor_scalar(out, in0, s1, s2, op0, op1), reads=rd, writes=[out])

    def stt(self, out, in0, scalar, in1, op0, op1, eng="dve"):
        rd = [in0, in1]
        if not isinstance(scalar, (int, float)):
            rd.append(scalar)
        return self.S.op(eng, lambda e: e.scalar_tensor_tensor(out, in0, scalar, in1, op0, op1), reads=rd, writes=[out])

    def copy(self, out, in_, eng="dve"):
        return self.S.op(eng, lambda e: e.tensor_copy(out, in_), reads=[in_], writes=[out])

    def memset(self, out, val, eng="dve"):
        return self.S.op(eng, lambda e: e.memset(out, val), reads=[], writes=[out])

    def dma(self, q, out, in_, slot=None):
        if slot is None:
            slot = "d%d" % self.dma_n
            self.dma_n += 1
        return self.S.op(q, lambda e: e.dma_start(out=out, in_=in_), reads=[in_], writes=[out],
                         chan="dma:" + slot, inc=16)


TILES = [(0, 4), (4, 7), (7, 10), (10, 13), (13, 17)]
OFF_X = 0
OFF_CX = 69632
OFF_H = 77824
OFF_SQ = 94208
OFF_TMP = 102400
OFF_RSTD = 106496
OFF_CONST = 110592
OFF_ADA = 115200
OFF_SCR = 123392


def build(mode="ALL"):
    nc = bass.Bass("TRN2", target_bir_lowering=False)
    P = Prog(nc)
    S = P.S
    do0 = mode in ("ALL", "L0")
    do1 = mode in ("ALL", "L1")

    def din(name, shape):
        return nc.dram_tensor(name, shape, F32, kind="ExternalInput").ap()

    def dout(name, shape):
        return nc.dram_tensor(name, shape, F32, kind="ExternalOutput").ap()

    d_xT = din("xT", [128, NCH, NT])
    d_ctxT = din("ctxT", [128, NCH, NCTX])
    d_cT = din("cT", [128, NCH * 2])
    d_small = din("small", [128, 16 + 256 + 48 + 96 + 16 + 128])
    d_ada = din("ada_r", [2, 24, 128, 2048])
    d_w1 = din("w1_r", [2, 16, 128, 2048])
    d_w2 = din("w2_r", [2, 8, 128, 4096])
    if do0:
        d_win = din("win_r", [128, 8192])
        d_wout = din("wout_r", [128, 8192])
        d_wgrp = din("wgrp_r", [128, 2048])
    if do1:
        d_wqkv = din("wqkv_r", [128, 8 * 1280])
        d_wo = din("wo_r", [128, 8192])
        d_rope = din("rope", [128, 2 * NT])
        d_mask = din("masks", [128, 384])
        d_out = dout("outT", [128, NCH, NT])
    else:
        d_x1 = dout("x1T", [128, NCH, NT])
        d_c1 = dout("c1T", [128, NCH, NCTX])

    X = P.view(OFF_X, [NCH, NT], F32)
    CX = P.view(OFF_CX, [NCH, NCTX], F32)
    co = [OFF_CONST]

    def calloc(shape, dt):
        n = 1
        for s_ in shape:
            n *= s_
        off = co[0]
        co[0] += (n * ESZ[dt] + 3) // 4 * 4
        assert co[0] <= OFF_ADA
        return P.view(off, shape, dt)

    SMALL = calloc([560], F32)
    PADV = SMALL[:, 0:16]
    CORR = SMALL[:, 16:272].rearrange("p (a b c d) -> p a b c d", a=2, b=2, c=8)
    GVEC = SMALL[:, 272:320].rearrange("p (a b) -> p a b", a=6)
    ADAB = SMALL[:, 320:416].rearrange("p (a b) -> p a b", a=2)
    SINK = SMALL[:, 416:432]
    XPAD = SMALL[:, 432:560].rearrange("p (a b) -> p a b", a=8)
    MOD = calloc([2, 48, 2], F32)
    AV = calloc([2, 2, 2, 8], F32)
    ONESM = calloc([128], BF16)
    ONES1 = calloc([128], BF16)
    CTF = calloc([16], F32)
    CTS = calloc([NCH, 2], BF16)
    ESK = calloc([16], F32)
    EPSV = calloc([1], F32)
    CORR1 = calloc([1], F32)

    cload = P.dma("sp", SMALL, d_small, slot="small")
    P.dma("sp", CTF, d_cT, slot="ct")
    P.memset(ONESM, 1.0 / 1024.0)
    P.memset(ONES1, 1.0)
    P.memset(EPSV, EPS)

    b0, b1 = TILES[0]
    P.dma("sp", X[:, :, b0 * 128:b1 * 128], d_xT[:, :, b0 * 128:b1 * 128])

    def late_x_loads(dep):
        for (b0, b1) in TILES[1:]:
            P.S.op("sp", (lambda b0=b0, b1=b1: (lambda e: e.dma_start(out=X[:, :, b0 * 128:b1 * 128], in_=d_xT[:, :, b0 * 128:b1 * 128])))(),
                   reads=[], writes=[X[:, :, b0 * 128:b1 * 128]], chan="dma:x%d" % b0, inc=16, extra_deps=dep)
        P.S.op("sp", lambda e: e.dma_start(out=CX, in_=d_ctxT), reads=[], writes=[CX], chan="dma:cx", inc=16, extra_deps=dep)

    P.act(CTS.rearrange("p a b -> p (a b)"), CTF, AF.Silu)

    hslot = [0]

    def Hview(N):
        k = hslot[0]
        hslot[0] ^= 1
        return P.view(OFF_H + k * 8192, [NCH, N], BF16)

    tslot = [0]

    def TMPv(N):
        k = tslot[0]
        tslot[0] ^= 1
        return P.view(OFF_TMP + k * 2048, [N], F32)

    rslot = [0]

    def RSv(N):
        k = rslot[0]
        rslot[0] ^= 1
        return P.view(OFF_RSTD + k * 2048, [N], F32)

    adaslot = [0]

    bgq = []
    if do0:
        bgq += [(0, j) for j in range(24)]
    if do1:
        bgq += [(1, j) for j in range(24)]
    bgs = {"dma": 0, "cmp": 0}
    mb = P.psum[7]

    def bg_slot(k):
        if k < 8:
            return OFF_SCR + k * 4096, "adas%d" % k
        return OFF_ADA + (k % 2) * 4096, "ada%d" % (k % 2)

    def bg_dma(k):
        l, j = bgq[k]
        off_, nm_ = bg_slot(k)
        slot2 = P.view(off_, [2048], BF16)
        P.dma("pool", slot2, d_ada[l, j], slot=nm_)

    def bg_cmp(k):
        l, j = bgq[k]
        slot3 = P.view(bg_slot(k)[0], [NCH, 256], BF16)
        for mi in range(2):
            m = 2 * j + mi
            for kc in range(NCH):
                P.mm(mb[:, 2 * m:2 * m + 2], slot3[:, kc, mi * 128:(mi + 1) * 128], CTS[:, kc, :],
                     start=(kc == 0), stop=(kc == NCH - 1))
        m0 = 2 * j
        op_ = P.tt(MOD[:, l, m0:m0 + 2, :], mb[:, 2 * m0:2 * m0 + 4].rearrange("p (a b) -> p a b", a=2),
                   ADAB[:, l, m0:m0 + 2].unsqueeze(2).to_broadcast([128, 2, 2]), ALU.add)
        if k == 4:
            late_x_loads([op_])
        for n in range(2):
            if j == (1 + 3 * n) * 4 + 3:
                for col in range(2):
                    P.stt(AV[:, l, n, col, :], MOD[:, l, (1 + 3 * n) * 8:(2 + 3 * n) * 8, col], 1.0,
                          GVEC[:, 2 * n + l, :], ALU.add, ALU.mult)

    def bg_step(n=1):
        for _ in range(n):
            k = bgs["cmp"]
            if k >= len(bgq):
                return
            nxt = k + 1
            if nxt >= 8 and nxt < len(bgq) and bgs["dma"] == nxt:
                bg_dma(nxt)
                bgs["dma"] += 1
            while bgs["dma"] <= k:
                bg_dma(bgs["dma"])
                bgs["dma"] += 1
            bg_cmp(k)
            bgs["cmp"] += 1

    def bg_ensure(l, which):
        tgt = bgq.index((l, which * 4 + 3)) + 1
        while bgs["cmp"] < tgt:
            bg_step()

    for k_ in range(8):
        bg_dma(k_)
    bgs["dma"] = 8

    def norm_sq(xsrc, N):
        sq = P.view(OFF_SQ, [NCH, N], BF16)
        P.act(sq, xsrc, AF.Square)
        return sq

    def norm_stat(sq, N, rs=None, bank=None):
        bk = bank if bank is not None else P.bank()
        for c in range(NCH):
            P.mm(bk[:, 0:N], ONESM, sq[:, c, :], start=(c == 0), stop=(c == NCH - 1))
        if rs is None:
            rs = RSv(N)
        P.act(rs, bk[:, 0:N], AF.Ln, bias=EPSV)
        P.act(rs, rs, AF.Exp, scale=-0.5)
        return rs

    def norm_apply(xsrc, N, rs, l, n, col, hdst, final=False):
        for c in range(NCH):
            tmp = TMPv(N)
            P.tt(tmp, xsrc[:, c, :], rs, ALU.mult)
            if final:
                P.act(hdst[:, c, :], tmp, AF.Identity, scale=GVEC[:, 4, c:c + 1])
            else:
                P.act(hdst[:, c, :], tmp, AF.Identity, scale=AV[:, l, n, col, c:c + 1],
                      bias=MOD[:, l, (3 * n) * 8 + c, col:col + 1])

    def norm_mod(xsrc, N, l, n, col, hdst, final=False):
        sq = norm_sq(xsrc, N)
        rs = norm_stat(sq, N)
        norm_apply(xsrc, N, rs, l, n, col, hdst, final)

    def norm_pieces(xsrc, N, l, n, col, hdst):
        stt_ = {}

        def p_sq():
            stt_["sq"] = norm_sq(xsrc, N)

        def p_st():
            stt_["rs"] = norm_stat(stt_["sq"], N)

        def p_ap(c):
            def f():
                tmp = TMPv(N)
                P.tt(tmp, xsrc[:, c, :], stt_["rs"], ALU.mult)
                P.act(hdst[:, c, :], tmp, AF.Identity, scale=AV[:, l, n, col, c:c + 1],
                      bias=MOD[:, l, (3 * n) * 8 + c, col:col + 1])
            return f
        return [p_sq, p_st] + [p_ap(c) for c in range(NCH)]

    def interleave(main, side, every=1, lead=0):
        side = list(side)
        for _ in range(min(lead, len(side))):
            side.pop(0)()
        for i_, m_ in enumerate(main):
            m_()
            if side and (i_ % every) == every - 1:
                side.pop(0)()
        while side:
            side.pop(0)()

    def mlp_prep(group):
        GT = sum(N for _, N, _ in group)
        assert GT <= 896
        offs = []
        o = 0
        for xv, N, col in group:
            offs.append(o)
            o += N
        return dict(group=group, GT=GT, offs=offs,
                    H2=P.view(OFF_H, [NCH, GT], BF16), HID=P.view(OFF_SCR, [NHC, GT], BF16))

    def mlp_norm(l, st):
        for (xv, N, col), o in zip(st["group"], st["offs"]):
            norm_mod(xv, N, l, 1, col, st["H2"][:, :, o:o + N])

    def mlp_1(l, st, hooks=None):
        o_w1 = OFF_SCR + NHC * 896 * 2
        H2, HID = st["H2"], st["HID"]
        for j in range(16):
            k = j % 3
            w2d = P.view(o_w1 + k * 4096, [2048], BF16)
            w3d = P.view(o_w1 + k * 4096, [NCH, 256], BF16)
            P.dma("pool", w2d, d_w1[l, j], slot="w1_%d" % k)
            for hh in range(2):
                hc = 2 * j + hh
                for (xv, N, col), o in zip(st["group"], st["offs"]):
                    bk = P.bank()
                    for kc in range(NCH):
                        P.mm(bk[:, 0:N], w3d[:, kc, hh * 128:(hh + 1) * 128], H2[:, kc, o:o + N],
                             start=(kc == 0), stop=(kc == NCH - 1))
                    tmp = TMPv(N)
                    P.act(tmp, bk[:, 0:N], AF.Relu)
                    P.tt(HID[:, hc, o:o + N], tmp, tmp, ALU.mult)
            if hooks and j in hooks:
                for f_ in hooks[j]:
                    f_()

    def mlp_2(l, st):
        o_w2 = OFF_SCR + NHC * 896 * 2 + 3 * 4096
        assert o_w2 + 2 * 8192 <= ARENA_BYTES
        HID = st["HID"]
        for fo in range(NCH):
            k = fo % 2
            w2d = P.view(o_w2 + k * 8192, [4096], BF16)
            w3d = P.view(o_w2 + k * 8192, [NHC, 128], BF16)
            P.dma("pool", w2d, d_w2[l, fo], slot="w2_%d" % k)
            for (xv, N, col), o in zip(st["group"], st["offs"]):
                bk = P.bank()
                for hc in range(NHC):
                    P.mm(bk[:, 0:N], w3d[:, hc, :], HID[:, hc, o:o + N], start=(hc == 0), stop=(hc == NHC - 1))
                P.stt(xv[:, fo, :], bk[:, 0:N], MOD[:, l, 40 + fo, col:col + 1], xv[:, fo, :], ALU.mult, ALU.add)
            bg_step()

    def staged_norm_hooks(l, st_next, hooks, first_j=1, bank=None):
        applies = []
        pend = {}
        tiles = list(zip(st_next["group"], st_next["offs"]))
        for ti_, ((xv, N, col), o) in enumerate(tiles):
            j_sq = first_j + 4 * ti_
            j_st = first_j + 4 * ti_ + 4

            def f_sq(xv=xv, N=N, ti_=ti_):
                pend[ti_] = norm_sq(xv, N)

            def f_st(N=N, ti_=ti_):
                pend[ti_] = norm_stat(pend[ti_], N, bank=bank)

            hooks.setdefault(j_sq, []).append(f_sq)
            hooks.setdefault(j_st, []).insert(0, f_st)

            def f_ap(xv=xv, N=N, col=col, o=o, ti_=ti_):
                norm_apply(xv, N, pend[ti_], l, 1, col, st_next["H2"][:, :, o:o + N])

            applies.append(f_ap)
        return applies

    def mlp_layer(l, groups, final_fn=None, first_staged=None):
        sts = [mlp_prep(g) for g in groups]
        if first_staged is None:
            mlp_norm(l, sts[0])
        else:
            for f_ in first_staged:
                f_()
        for gi, st in enumerate(sts):
            hooks = {}
            applies = []
            if gi + 1 < len(sts):
                applies = staged_norm_hooks(l, sts[gi + 1], hooks)
            if final_fn is not None and gi > 0:
                final_fn(gi - 1, hooks)
            mlp_1(l, st, hooks)
            for f_ in applies:
                f_()
            mlp_2(l, st)
        if final_fn is not None:
            final_fn(len(sts) - 1, None)

    lat_tiles = [(X[:, :, b0 * 128:b1 * 128], (b1 - b0) * 128, 0) for (b0, b1) in TILES]

    if do0:
        bg_ensure(0, 1)
        o = OFF_SCR
        UL = P.view(o, [NCH, NT + 16], BF16); o += NCH * (NT + 16) * 2
        UC = P.view(o, [NCH, NCTX + 16], BF16); o += NCH * (NCTX + 16) * 2
        o_win = o
        WINU = [P.view(o + u_ * 4096, [NCH, 256], BF16) for u_ in range(4)]
        WINU2 = [P.view(o + u_ * 4096, [2048], BF16) for u_ in range(4)]
        DTS = [P.view(o, [NCH, 512], BF16), P.view(OFF_H, [NCH, 512], BF16)]
        Y1S = [P.view(o + 8192, [NCH, 512], BF16), P.view(OFF_H + 8192, [NCH, 512], BF16)]
        o += 16384
        WOUT2 = P.view(o, [8192], BF16); WOUT = P.view(o, [NCH, 1024], BF16); o += 16384
        WG2 = P.view(o, [2048], BF16); WG = P.view(o, [4, 2, 256], BF16); o += 4096
        WGS = P.view(o, [4, 2, 256], BF16); o += 4096
        PT_ = [P.view(o + i * 2112, [2, 528], BF16) for i in range(3)]; o += 3 * 2112
        PTP_ = [P.view(o + i * 2112, [2, 528], BF16) for i in range(1)]; o += 1 * 2112
        assert o <= ARENA_BYTES, o
        for u_ in range(4):
            P.dma("pool", WINU2[u_], d_win[:, u_ * 2048:(u_ + 1) * 2048], slot="win%d" % u_)
        P.memset(UC[:, :, 0:8], 0.0)
        P.memset(UC[:, :, NCTX + 8:NCTX + 16], 0.0)

        ph1 = [(xv, N, col, UL, 8 + b0 * 128) for (xv, N, col), (b0, b1) in zip(lat_tiles, TILES)]
        ph1.insert(1, (XPAD, 16, 0, UL, None))
        ph1.append((CX, NCTX, 1, UC, 8))
        hs = {}

        def p1_norm_pieces(i):
            xv, N, col, UB, ucol = ph1[i]
            hs[i] = Hview(N)
            return norm_pieces(xv, N, 0, 0, col, hs[i])

        def p1_proj_pieces(i):
            xv, N, col, UB, ucol = ph1[i]
            h = hs[i]

            def piece(cc):
                def f():
                    bk = P.bank()
                    for kc in range(NCH):
                        P.mm(bk[:, 0:N], WINU[cc // 2][:, kc, (cc % 2) * 128:(cc % 2) * 128 + 128], h[:, kc, :], start=(kc == 0), stop=(kc == NCH - 1))
                    if ucol is None:
                        P.tt(UB[:, cc, 0:8], bk[:, 0:8], PADV[:, 0:8], ALU.mult)
                        P.tt(UB[:, cc, NT + 8:NT + 16], bk[:, 8:16], PADV[:, 8:16], ALU.mult)
                    else:
                        P.copy(UB[:, cc, ucol:ucol + N], bk[:, 0:N])
                return f
            return [piece(cc) for cc in range(NCH)]

        for f_ in p1_norm_pieces(0):
            f_()
        for i in range(len(ph1)):
            side = p1_norm_pieces(i + 1) if i + 1 < len(ph1) else []
            interleave(p1_proj_pieces(i), side, lead=2)
            bg_step()
            if i == 1:
                P.dma("pool", WG2, d_wgrp, slot="wg")
                P.dma("pool", WOUT2, d_wout, slot="wout")
                for g_, w_ in enumerate((2, 4, 8, 16)):
                    P.ts(WGS[:, g_], WG[:, g_], 1.0 / w_, None, ALU.mult)
                P.ts(WG2, WG2, -1.0, None, ALU.mult)
        bg_ensure(0, 2)

        ph2 = [(xv, N, col, UL, 8 + b0 * 128, 0, b0 == 0, b1 == NBLK) for (xv, N, col), (b0, b1) in zip(lat_tiles, TILES)]
        ph2.append((CX, NCTX, 1, UC, 8, 1, True, True))

        def p2_pool_g(i, g, w):
            xv, N, col, UB, base, ci, first, last = ph2[i]
            DT = DTS[i % 2]
            if True:
                cs = slice(2 * g, 2 * g + 2)
                en = "pool" if g < 2 else "dve"
                if g < 2:
                    A_ = PTP_[0]
                    B_ = C_ = None
                else:
                    A_, B_, C_ = PT_
                dst = DT[:, cs, 0:N]

                def u(a, n):
                    return UB[:, cs, base + a:base + a + n]
                if w == 2:
                    P.tt(dst, u(-1, N), u(0, N), ALU.add, eng=en)
                elif w == 4:
                    P.tt(A_[:, :, 0:N + 2], u(-2, N + 2), u(-1, N + 2), ALU.add, eng=en)
                    P.tt(dst, A_[:, :, 0:N], A_[:, :, 2:N + 2], ALU.add, eng=en)
                elif w == 8:
                    P.tt(A_[:, :, 0:N + 6], u(-4, N + 6), u(-3, N + 6), ALU.add, eng=en)
                    P.tt(B_[:, :, 0:N + 4], A_[:, :, 0:N + 4], A_[:, :, 2:N + 6], ALU.add, eng=en)
                    P.tt(dst, B_[:, :, 0:N], B_[:, :, 4:N + 4], ALU.add, eng=en)
                else:
                    P.tt(A_[:, :, 0:N + 14], u(-8, N + 14), u(-7, N + 14), ALU.add, eng=en)
                    P.tt(B_[:, :, 0:N + 12], A_[:, :, 0:N + 12], A_[:, :, 2:N + 14], ALU.add, eng=en)
                    P.tt(C_[:, :, 0:N + 8], B_[:, :, 0:N + 8], B_[:, :, 4:N + 12], ALU.add, eng=en)
                    P.tt(dst, C_[:, :, 0:N], C_[:, :, 8:N + 8], ALU.add, eng=en)
                if first:
                    P.tt(DT[:, cs, 0:8], DT[:, cs, 0:8], CORR[:, ci, 0, cs, :], ALU.mult, eng=en)
                if last:
                    P.tt(DT[:, cs, N - 8:N], DT[:, cs, N - 8:N], CORR[:, ci, 1, cs, :], ALU.mult, eng=en)

        def p2_pool_pieces(i):
            return [(lambda g=g, w=w: p2_pool_g(i, g, w)) for g, w in enumerate((2, 4, 8, 16))]

        def p2_mm_pieces(i):
            xv, N, col, UB, base, ci, first, last = ph2[i]
            DT = DTS[i % 2]
            Y1 = Y1S[i % 2]

            def grp(cc):
                def f():
                    g = cc // 2
                    bk = P.bank()
                    cw = slice((cc % 2) * 128, (cc % 2) * 128 + 128)
                    for k2 in range(2):
                        P.mm(bk[:, 0:N], WGS[:, g, k2, cw], DT[:, 2 * g + k2, 0:N], start=(k2 == 0), stop=False)
                    for k2 in range(2):
                        P.mm(bk[:, 0:N], WG[:, g, k2, cw], UB[:, 2 * g + k2, base:base + N], start=False, stop=(k2 == 1))
                    P.act(Y1[:, cc, 0:N], bk[:, 0:N], AF.Copy, scale=GVEC[:, 5, cc:cc + 1])
                return f

            def wout(cc):
                def f():
                    bk = P.bank()
                    for kc in range(NCH):
                        P.mm(bk[:, 0:N], WOUT[:, kc, cc * 128:(cc + 1) * 128], Y1[:, kc, 0:N], start=(kc == 0), stop=(kc == NCH - 1))
                    P.stt(xv[:, cc, :], bk[:, 0:N], MOD[:, 0, 16 + cc, col:col + 1], xv[:, cc, :], ALU.mult, ALU.add)
                return f
            return [grp(cc) for cc in range(NCH)] + [wout(cc) for cc in range(NCH)]

        ctx_tile = (CX, NCTX, 1)
        l0_groups = [[lat_tiles[0], lat_tiles[1]], [lat_tiles[2], lat_tiles[3]], [lat_tiles[4], ctx_tile]]
        st0 = mlp_prep(l0_groups[0])
        hk0 = {}
        first_applies = staged_norm_hooks(0, st0, hk0, first_j=3)
        for f_ in p2_pool_pieces(0):
            f_()
        stage_seq = []
        for j_ in sorted(hk0):
            stage_seq += hk0[j_]
        for i in range(len(ph2)):
            side = p2_pool_pieces(i + 1) if i + 1 < len(ph2) else []
            main = p2_mm_pieces(i)
            interleave(main[:8], side[:2], every=4)
            bg_step()
            interleave(main[8:], side[2:], every=4)
            bg_step()
            if i >= 2 and stage_seq:
                stage_seq.pop(0)()

        while stage_seq:
            stage_seq.pop(0)()
        bg_ensure(0, 5)
        mlp_layer(0, l0_groups, first_staged=first_applies)
        if do1:
            bg_ensure(1, 5)
    elif do1:
        bg_ensure(1, 5)

    if do1:
        o = OFF_ADA
        QT = P.view(o, [NCH, NT], BF16); o += NCH * NT * 2
        KTK = [P.view(o + i * (NT + NCTX) * 2, [NT + NCTX], BF16) for i in range(2)]; o += 2 * (NT + NCTX) * 2
        VA = [P.view(o + i * (NBLK + 2) * 256, [NBLK + 2, 128], BF16) for i in range(2)]; o += 2 * (NBLK + 2) * 256
        o_ph = o
        ROPE2 = P.view(o, [2 * NT], BF16); COS = P.view(o, [NT], BF16); SIN = P.view(o + NT * 2, [NT], BF16); o += 4 * NT
        WV = P.view(o, [NCH, 128], BF16); WV2 = P.view(o, [1024], BF16)
        WQC = [P.view(o + 2048 + c * 2048, [NCH, 128], BF16) for c in range(9)]
        WQC2 = [P.view(o + 2048 + c * 2048, [1024], BF16) for c in range(9)]
        o += 10 * 2048
        PERM = P.view(o, [128], BF16); o += 256
        QB = [P.view(o + i * 1024, [512], BF16) for i in range(2)]; o += 2048
        RTS = [P.view(o + i * 2048, [512], F32) for i in range(4)]; o += 8192
        rtc = [0]
        assert o <= ARENA_BYTES, o
        P.dma("pool", WQC2[0], d_wqkv[:, 1024:2048], slot="wq0")
        P.dma("pool", PERM, d_mask[:, 256:384], slot="perm")
        P.dma("pool", ROPE2, d_rope, slot="rope")
        for c in range(1, 9):
            P.dma("pool", WQC2[c], d_wqkv[:, 1024 + c * 1024:1024 + (c + 1) * 1024], slot="wq%d" % c)
        P.dma("pool", WV2, d_wqkv[:, 0:1024], slot="wv")
        P.memset(KTK[0][64:128, :], 0.0)
        P.memset(VA[0][:, :, 64:128], 1.0)
        P.memset(VA[1][:, :, 0:64], 1.0)
        P.memset(KTK[1][0:64, :], 0.0)
        P.act(ESK, SINK, AF.Exp)

        l1t = [(xv, N, 0, b0, b1) for (xv, N, col), (b0, b1) in zip(lat_tiles, TILES)] + [(CX, NCTX, 1, NBLK, NBLK + 2)]
        hs1 = {}

        def q_norm_pieces(i):
            xv, N, col, b0, b1 = l1t[i]
            hs1[i] = Hview(N)
            return norm_pieces(xv, N, 1, 0, col, hs1[i])

        def q_proj_pieces(i):
            xv, N, col, b0, b1 = l1t[i]
            h = hs1[i]
            t0 = b0 * 128
            pcs = []
            pend = {}

            def rope_fin(c, ba, qb_):
                bb = P.bank()
                P.mm(bb[:, 0:N], PERM, qb_[:, 0:N])
                t1 = RTS[rtc[0] % 4][:, 0:N]
                t2 = RTS[(rtc[0] + 1) % 4][:, 0:N]
                rtc[0] += 2
                P.tt(t1, ba[:, 0:N], COS[:, t0:t0 + N], ALU.mult)
                P.tt(t2, bb[:, 0:N], SIN[:, t0:t0 + N], ALU.mult)
                if c < NCH:
                    P.tt(QT[:, c, t0:t0 + N], t1, t2, ALU.add, eng="pool")
                else:
                    P.tt(KTK[0][0:64, t0:t0 + N], t1[0:64], t2[0:64], ALU.add, eng="pool")
                    P.tt(KTK[1][64:128, t0:t0 + N], t1[64:128], t2[64:128], ALU.add, eng="pool")

            def vpiece():
                nb = b1 - b0
                bv = P.bank()
                for bi in range(nb):
                    for kc in range(NCH):
                        P.mm(bv[:, bi * 128:(bi + 1) * 128], h[:, kc, bi * 128:(bi + 1) * 128], WV[:, kc, :],
                             start=(kc == 0), stop=(kc == NCH - 1))
                bv3 = bv[:, 0:nb * 128].rearrange("p (a b) -> p a b", a=nb)
                P.act(VA[0][:, b0:b1, 0:64], bv3[:, :, 0:64], AF.Copy)
                P.act(VA[1][:, b0:b1, 64:128], bv3[:, :, 64:128], AF.Copy)

            if col == 0:
                def qpiece(c):
                    def f():
                        ba = P.bank()
                        for kc in range(NCH):
                            P.mm(ba[:, 0:N], WQC[c][:, kc, :], h[:, kc, :], start=(kc == 0), stop=(kc == NCH - 1))
                        qb_ = QB[c % 2]
                        P.act(qb_[:, 0:N], ba[:, 0:N], AF.Copy)
                        if "p" in pend:
                            rope_fin(*pend["p"])
                        pend["p"] = (c, ba, qb_)
                    return f
                pcs = [qpiece(c) for c in range(NCH + 1)]

                def last():
                    rope_fin(*pend["p"])
                    vpiece()
                pcs.append(last)
            else:
                def kpiece():
                    ba = P.bank()
                    for kc in range(NCH):
                        P.mm(ba[:, 0:N], WQC[8][:, kc, :], h[:, kc, :], start=(kc == 0), stop=(kc == NCH - 1))
                    P.act(KTK[0][0:64, NT:NT + NCTX], ba[0:64, 0:N], AF.Copy)
                    P.act(KTK[1][64:128, NT:NT + NCTX], ba[64:128, 0:N], AF.Copy)
                pcs = [kpiece, vpiece]
            return pcs

        for f_ in q_norm_pieces(0):
            f_()
        for i in range(len(l1t)):
            side = q_norm_pieces(i + 1) if i + 1 < len(l1t) else []
            interleave(q_proj_pieces(i), side)

        o = o_ph
        WO2 = P.view(o, [8192], BF16); WO = P.view(o, [NCH, 1024], BF16); o += 16384
        PTS = [P.view(o + i * 1024, [4, 128], BF16) for i in range(10)]; o += 10 * 1024
        OTS = [P.view(OFF_H + i * 8192, [NCH, 512], BF16) for i in range(2)]
        DNS = [P.view(o + i * 2048, [4, 128], F32) for i in range(3)]; o += 3 * 2048
        RC2 = [P.view(o + i * 2048, [4, 128], F32) for i in range(3)]; o += 3 * 2048
        MSK2 = P.view(o, [256], BF16); MSK = P.view(o, [2, 128], BF16); o += 512
        assert o <= ARENA_BYTES
        P.dma("pool", MSK2, d_mask[:, 0:256], slot="mask")
        P.dma("pool", WO2, d_wo, slot="wo")
        SBK = [P.psum[i] for i in range(3)]
        OBK = [P.psum[4 + i] for i in range(4)]
        sctr = [0]

        units = [(qb, kvh, half) for qb in range(NBLK) for half in range(2) for kvh in range(2)]
        tile_of = {}
        for ti, (b0, b1) in enumerate(TILES):
            for qb in range(b0, b1):
                tile_of[qb] = ti

        def kbs_of(qb):
            kbs = []
            if qb > 0:
                kbs.append((qb - 1, 0))
            kbs.append((qb, None))
            if qb < NBLK - 1:
                kbs.append((qb + 1, 1))
            kbs.append((NBLK, None))
            kbs.append((NBLK + 1, None))
            return kbs

        def emit_S(ui, i):
            qb, kvh, half = units[ui]
            c0 = half * 4
            kb, mk = kbs_of(qb)[i]
            sb = SBK[sctr[0] % 3]
            sctr[0] += 1
            P.mm(sb[:, :], KTK[kvh][:, kb * 128:(kb + 1) * 128], QT[:, c0:c0 + 4, qb * 128:(qb + 1) * 128])
            pt = PTS[(ui % 2) * 5 + i]
            P.act(pt, sb[:, :].rearrange("p (a b) -> p a b", a=4), AF.Exp, scale=0.125)
            if mk is not None:
                P.tt(pt, pt, MSK[:, mk, :].unsqueeze(1).to_broadcast([128, 4, 128]), ALU.mult, eng="pool")

        def emit_PV(ui, i):
            qb, kvh, half = units[ui]
            kbs = kbs_of(qb)
            n = len(kbs)
            kb, mk = kbs[i]
            ob = OBK[ui % 4]
            pt = PTS[(ui % 2) * 5 + i].rearrange("p a b -> p (a b)")
            P.mm(ob[:, :], VA[kvh][:, kb, :], pt, start=(i == 0), stop=(i == n - 1))

        def emit_den(ui):
            qb, kvh, half = units[ui]
            c0 = half * 4
            drows = slice(64, 128) if kvh == 0 else slice(0, 64)
            ob = OBK[ui % 4]
            dn = DNS[(ui // 2) % 3]
            h0 = kvh * 8 + c0
            P.tt(dn[drows], ob[drows, :].rearrange("p (a b) -> p a b", a=4),
                 ESK[drows, h0:h0 + 4].unsqueeze(2).to_broadcast([64, 4, 128]), ALU.add)

        def emit_recip(pi):
            dn = DNS[pi % 3]
            r2 = RC2[pi % 3]
            P.act(dn, dn, AF.Ln)
            P.act(dn, dn, AF.Exp, scale=-1.0)
            P.dma("sp", r2[0:64], dn[64:128], slot="rsw%da" % (pi % 3))
            P.dma("sp", r2[64:128], dn[0:64], slot="rsw%db" % (pi % 3))

        def emit_mult(ui):
            qb, kvh, half = units[ui]
            rows = slice(kvh * 64, kvh * 64 + 64)
            c0 = half * 4
            ti = tile_of[qb]
            b0, b1 = TILES[ti]
            qoff = (qb - b0) * 128
            ob = OBK[ui % 4]
            r2 = RC2[(ui // 2) % 3]
            P.tt(OTS[ti % 2][rows, c0:c0 + 4, qoff:qoff + 128], ob[rows, :].rearrange("p (a b) -> p a b", a=4),
                 r2[rows], ALU.mult)

        deferred = []

        def wo_closure(ti, fo):
            def f():
                xv, N, col = lat_tiles[ti]
                bk = P.psum[3]
                for c in range(NCH):
                    P.mm(bk[:, 0:N], WO[:, c, fo * 128:(fo + 1) * 128], OTS[ti % 2][:, c, 0:N], start=(c == 0), stop=(c == NCH - 1))
                P.stt(xv[:, fo, :], bk[:, 0:N], MOD[:, 1, 16 + fo, 0:1], xv[:, fo, :], ALU.mult, ALU.add)
            return f

        def finish_pair(pi):
            ua, ub = 2 * pi, 2 * pi + 1
            emit_mult(ua)
            emit_mult(ub)
            qb = units[ub][0]
            ti = tile_of[qb]
            b0, b1 = TILES[ti]
            if qb == b1 - 1 and units[ub][2] == 1:
                for fo in range(NCH):
                    deferred.append(wo_closure(ti, fo))

        L1M = [(0, 3), (3, 6), (6, 9), (9, 12), (12, 15), (15, 17)]
        m_tiles = [(X[:, :, b0 * 128:b1 * 128], (b1 - b0) * 128, 0) for (b0, b1) in L1M]
        l1_groups = [[m_tiles[0], m_tiles[1]], [m_tiles[2], m_tiles[3]], [m_tiles[4], m_tiles[5]]]
        st1 = mlp_prep(l1_groups[0])
        att_hooks = {}
        l1_first_applies = staged_norm_hooks(1, st1, att_hooks, first_j=40, bank=P.psum[3])

        for i in range(len(kbs_of(units[0][0]))):
            emit_S(0, i)
        for ui, (qb, kvh, half) in enumerate(units):
            nP = len(kbs_of(qb))
            nS = len(kbs_of(units[ui + 1][0])) if ui + 1 < len(units) else 0
            for i in range(max(nP, nS)):
                if i < nS:
                    emit_S(ui + 1, i)
                if i < nP:
                    emit_PV(ui, i)
            emit_den(ui)
            if ui % 2 == 1:
                emit_recip(ui // 2)
            elif ui >= 2:
                finish_pair(ui // 2 - 1)
            if deferred:
                deferred.pop(0)()
            for f_ in att_hooks.get(ui, []):
                f_()
        finish_pair(len(units) // 2 - 1)
        while deferred:
            deferred.pop(0)()
        P.bank_rr = 0

        outs = []
        fin = {0: [0, 1], 1: [2, 3], 2: [4, 5]}
        FRS = [P.view(OFF_ADA + i * 2048, [512], F32) for i in range(2)]

        def final_fn(gi, hooks):
            tis = fin[gi]
            pend = {}
            for k_, ti in enumerate(tis):
                xv, N, col = m_tiles[ti]
                b0, b1 = L1M[ti]

                def f_sq(xv=xv, N=N, k_=k_):
                    pend[k_] = norm_sq(xv, N)

                def f_fin(xv=xv, N=N, k_=k_, b0=b0, b1=b1):
                    rs = norm_stat(pend[k_], N, rs=FRS[k_][:, 0:N])
                    norm_apply(xv, N, rs, 0, 0, 0, xv, final=True)
                    outs.append(P.dma("sp", d_out[:, :, b0 * 128:b1 * 128], xv))

                if hooks is None:
                    f_sq()
                    f_fin()
                else:
                    hooks.setdefault(10 + 3 * k_, []).append(f_sq)
                    hooks.setdefault(12 + 3 * k_, []).append(f_fin)

        mlp_layer(1, l1_groups, final_fn=final_fn, first_staged=l1_first_applies)
        S.op("sp", None, extra_deps=outs)
    else:
        outs = []
        for (xv, N, col), (b0, b1) in zip(lat_tiles, TILES):
            outs.append(P.dma("sp", d_x1[:, :, b0 * 128:b1 * 128], xv))
        outs.append(P.dma("sp", d_c1, CX))
        S.op("sp", None, extra_deps=outs)

    S.emit()
    return nc


POOL_WINDOWS = (2, 4, 8, 16)


def _fm(a):
    T = a.shape[0]
    return np.ascontiguousarray(a.reshape(T, NCH, 128).transpose(2, 1, 0))


def _wl(w):
    K, N = w.shape
    return np.ascontiguousarray(w.reshape(K // 128, 128, N).transpose(1, 0, 2).reshape(128, (K // 128) * N))


def _corr_tables(t0g, n_main, L):
    c = np.ones((2, NCH, 8), np.float32)
    for g, w in enumerate(POOL_WINDOWS):
        for side in range(2):
            for i in range(8):
                t = t0g + i if side == 0 else t0g + n_main - 8 + i
                lo = min(max(t - w // 2, 0), L)
                hi = min(max(t + w // 2, 0), L)
                c[side, 2 * g:2 * g + 2, i] = np.float32(w) / np.float32(hi - lo)
    return c


def _rope_tables(t0g):
    t = np.arange(t0g, t0g + NT)
    row = (t // 64).astype(np.float32)
    colp = (t % 64).astype(np.float32)
    inv = (np.float32(10000.0) ** (-np.arange(0, 32, 2, dtype=np.float32) / np.float32(32))).astype(np.float32)
    cos = np.zeros((128, NT), np.float32)
    sin = np.zeros((128, NT), np.float32)
    for p in range(128):
        d = p % 64
        pos = row if d < 32 else colp
        dd = d % 32
        ang = (pos * inv[dd % 16]).astype(np.float32)
        cos[p] = np.cos(ang)
        sin[p] = -np.sin(ang) if dd < 16 else np.sin(ang)
    return np.concatenate([cos, sin], axis=1)


def _shared_weights(inp):
    f = np.float32
    sh = {}
    ada_w = np.asarray(inp["ada_w"], f)
    sh["ada_r"] = np.ascontiguousarray(ada_w.reshape(2, NCH, 128, 24, 256).transpose(0, 3, 2, 1, 4).reshape(2, 24, 128, 2048))
    w1 = np.asarray(inp["mlp_w1"], f)
    sh["w1_r"] = np.ascontiguousarray(w1.reshape(2, NCH, 128, 16, 256).transpose(0, 3, 2, 1, 4).reshape(2, 16, 128, 2048))
    w2 = np.asarray(inp["mlp_w2"], f)
    sh["w2_r"] = np.ascontiguousarray(w2.reshape(2, NHC, 128, NCH, 128).transpose(0, 3, 2, 1, 4).reshape(2, NCH, 128, 4096))
    win = np.asarray(inp["pool_w_in"], f)[0]
    sh["win_r"] = np.concatenate([_wl(win[:, u * 256:(u + 1) * 256]) for u in range(4)], axis=1)
    sh["wout_r"] = _wl(np.asarray(inp["pool_w_out"], f)[0])
    wg = np.asarray(inp["pool_w_grp"], f)[0]
    sh["wgrp_r"] = np.ascontiguousarray(wg.reshape(4, 2, 128, 256).transpose(2, 0, 1, 3).reshape(128, 2048))
    wqkv = np.asarray(inp["attn_w_qkv"], f)[0]
    hd_order = []
    for c in range(8):
        hd_order += [c, 8 + c]
    qcols = np.concatenate([np.arange(h * 64, h * 64 + 64) for h in hd_order])
    partner = np.array([(d + 16) if (d % 32) < 16 else (d - 16) for d in range(64)])
    qpcols = np.concatenate([h * 64 + partner for h in hd_order])
    kcols = 1024 + np.arange(128)
    kpcols = 1024 + np.concatenate([h * 64 + partner for h in range(2)])
    vcols = 1152 + np.arange(128)
    units = [vcols]
    for c in range(8):
        units.append(qcols[c * 128:(c + 1) * 128])
    units.append(kcols)
    sh["wqkv_r"] = np.concatenate([_wl(wqkv[:, u]) for u in units], axis=1)
    wo = np.asarray(inp["attn_w_o"], f)[0]
    sh["wo_r"] = _wl(wo[qcols, :])
    ml = np.zeros((128, 2, 128), f)
    jj = np.arange(128)[:, None]
    ii = np.arange(128)[None, :]
    ml[:, 0, :] = (jj >= ii)
    ml[:, 1, :] = (jj <= ii)
    pm = np.zeros((128, 128), f)
    for m in range(128):
        pm[(m // 64) * 64 + partner[m % 64], m] = 1.0
    sh["masks"] = np.concatenate([ml.reshape(128, 256), pm], axis=1)
    return sh


def _core_inputs(inp, core):
    f = np.float32
    b, hf = core // 2, core % 2
    x = np.asarray(inp["x"], f)[b]
    t0g = 0 if hf == 0 else SEQ - NT
    d = {}
    d["xT"] = _fm(x[t0g:t0g + NT])
    d["ctxT"] = _fm(np.asarray(inp["ctx"], f)[b])
    cc = np.stack([np.asarray(inp["c"], f)[b], np.asarray(inp["c_ctx"], f)], axis=1)
    d["cT"] = np.ascontiguousarray(cc.reshape(NCH, 128, 2).transpose(1, 0, 2).reshape(128, 16))
    small = np.zeros((128, 560), f)
    xpad = np.zeros((16, D), f)
    if hf == 0:
        small[:, 8:16] = 1.0
        xpad[8:16] = x[NT:NT + 8]
    else:
        small[:, 0:8] = 1.0
        xpad[0:8] = x[t0g - 8:t0g]
    corr = np.stack([_corr_tables(t0g, NT, SEQ), _corr_tables(0, NCTX, NCTX)], axis=0)
    small[:, 16:272] = corr.reshape(1, 256)
    gv = np.stack([np.asarray(inp["norm_mix_g"], f)[0], np.asarray(inp["norm_mix_g"], f)[1],
                   np.asarray(inp["norm_mlp_g"], f)[0], np.asarray(inp["norm_mlp_g"], f)[1],
                   np.asarray(inp["final_g"], f), np.asarray(inp["pool_scale"], f)[0]], axis=0)
    small[:, 272:320] = gv.reshape(6, NCH, 128).transpose(2, 0, 1).reshape(128, 48)
    ab = np.asarray(inp["ada_b"], f)
    small[:, 320:416] = ab.reshape(2, 48, 128).transpose(2, 0, 1).reshape(128, 96)
    small[:, 416:432] = np.asarray(inp["attn_sink"], f)[0][None, :]
    small[:, 432:560] = _fm(xpad).reshape(128, 128)
    d["small"] = small
    d["rope"] = _rope_tables(t0g)
    return d


def _run(mode, shared, percore):
    nc = build(mode)
    names_by_mode = {
        "ALL": ["ada_r", "w1_r", "w2_r", "win_r", "wout_r", "wgrp_r", "wqkv_r", "wo_r", "masks"],
        "L0": ["ada_r", "w1_r", "w2_r", "win_r", "wout_r", "wgrp_r"],
        "L1": ["ada_r", "w1_r", "w2_r", "wqkv_r", "wo_r", "masks"],
    }[mode]
    in_maps = []
    for c in range(8):
        m = {k: shared[k] for k in names_by_mode}
        for k, v in percore[c].items():
            if k == "rope" and mode == "L0":
                continue
            m[k] = v
        in_maps.append(m)
    return run_bass_kernel_spmd(nc, in_maps, core_ids=list(range(8))).results


FUSED = True


def kernel(**inputs):
    shared = _shared_weights(inputs)
    percore = [_core_inputs(inputs, c) for c in range(8)]
    if FUSED:
        res = _run("ALL", shared, percore)
    else:
        r0 = _run("L0", shared, percore)
        for c in range(8):
            percore[c]["xT"] = r0[c]["x1T"]
            percore[c]["ctxT"] = r0[c]["c1T"]
        res = _run("L1", shared, percore)
    out = np.zeros((NB, SEQ, D), np.float32)
    for c in range(8):
        b, hf = c // 2, c % 2
        o = res[c]["outT"]
        tok = o.transpose(2, 1, 0).reshape(NT, D)
        if hf == 0:
            out[b, 0:2048] = tok[0:2048]
        else:
            out[b, 2048:4096] = tok[NT - 2048:NT]
    return out
```

```python
import numpy as np
from contextlib import ExitStack
import concourse.bass as bass
import concourse.mybir as mybir
from concourse.bass_utils import run_bass_kernel_spmd

F32 = mybir.dt.float32
BF16 = mybir.dt.bfloat16
AF = mybir.ActivationFunctionType
ALU = mybir.AluOpType
ESZ = {F32: 4, BF16: 2}

D = 1024
NCH = 8
SEQ = 4096
NB = 4
NT = 2176
NBLK = 17
NCTX = 256
DFF = 4096
NHC = 32
EPS = 1e-6
ARENA_BYTES = 212800
PSUM_BANK = 2048
STRICT_SAME_ENGINE = True


class _Op:
    __slots__ = ("id", "eng", "chan", "fn", "deps", "signal", "waits", "pos", "inc", "sigval")


class Sched:
    BK = 2048

    def __init__(self, nc, psum_names):
        self.nc = nc
        self.ops = []
        self.streams = {k: [] for k in ("pe", "act", "dve", "pool", "sp")}
        self.chan_ops = {}
        self.buckets = {}
        self.psum_names = psum_names

    def rects(self, ap):
        t = ap.tensor
        name = t.name
        if name in self.psum_names:
            b = self.psum_names[name]
            return "ps", [(0, 128, b * PSUM_BANK, (b + 1) * PSUM_BANK)]
        if name != "arena":
            return None, []
        esz = ESZ[ap.dtype]
        dims = [tuple(d) for d in ap.ap]
        pstride, pcount = dims[0]
        off = int(ap.offset)
        if pstride > 0:
            p0 = off // pstride
            foff = off % pstride
        else:
            p0 = 0
            foff = off
        p1 = p0 + pcount
        free = dims[1:]
        if not free:
            return "sb", [(p0, p1, foff * esz, (foff + 1) * esz)]
        ls, ln = free[-1]
        run = ((ln - 1) * abs(ls) + 1)
        outer = free[:-1]
        tot = 1
        for s, n in outer:
            tot *= n
        res = []
        if tot <= 64:
            idxs = [0]
            for s, n in outer:
                idxs = [b + s * i for b in idxs for i in range(n)]
            for b in idxs:
                st = foff + b
                res.append((p0, p1, st * esz, (st + run) * esz))
        else:
            mx = foff + sum(s * (n - 1) for s, n in outer)
            res.append((p0, p1, foff * esz, (mx + run) * esz))
        return "sb", res

    def _query_insert(self, op, accs):
        deps = {}
        newrecs = []
        for space, r, isw in accs:
            p0, p1, b0, b1 = r
            seen = set()
            for bk in range(b0 // self.BK, (b1 - 1) // self.BK + 1):
                lst = self.buckets.get((space, bk))
                if not lst:
                    continue
                for rec in lst:
                    if not rec[7] or id(rec) in seen:
                        continue
                    seen.add(id(rec))
                    if rec[2] < b1 and b0 < rec[3] and rec[0] < p1 and p0 < rec[1]:
                        if rec[5] or isw:
                            kind = "RAW" if (rec[5] and not isw) else "W"
                            prev = deps.get(rec[4])
                            if prev is None or kind == "RAW":
                                deps[rec[4]] = kind
                        if isw and p0 <= rec[0] and rec[1] <= p1 and b0 <= rec[2] and rec[3] <= b1:
                            rec[7] = False
                        elif (not isw) and (not rec[5]) and rec[6] == op.chan and rec[0] == p0 and rec[1] == p1 \
                                and rec[2] == b0 and rec[3] == b1:
                            rec[7] = False
            newrecs.append((space, [p0, p1, b0, b1, op.id, isw, op.chan, True]))
        for space, rec in newrecs:
            for bk in range(rec[2] // self.BK, (rec[3] - 1) // self.BK + 1):
                lst = self.buckets.setdefault((space, bk), [])
                lst.append(rec)
                if len(lst) > 96:
                    lst[:] = [x for x in lst if x[7]]
        deps.pop(op.id, None)
        return deps

    def op(self, eng, fn, reads=(), writes=(), chan=None, inc=1, extra_deps=()):
        o = _Op()
        o.id = len(self.ops)
        o.eng = eng
        o.chan = chan or eng
        o.fn = fn
        o.inc = inc
        o.signal = chan is not None
        o.waits = []
        o.sigval = 0
        accs = []
        for ap in reads:
            if ap is None or isinstance(ap, (int, float)):
                continue
            sp, rs = self.rects(ap)
            for r in rs:
                accs.append((sp, r, sp == "ps"))
        for ap in writes:
            sp, rs = self.rects(ap)
            for r in rs:
                accs.append((sp, r, True))
        o.deps = self._query_insert(o, accs)
        for d in extra_deps:
            o.deps[d.id] = "RAW"
        self.ops.append(o)
        self.streams[eng].append(o)
        lst = self.chan_ops.setdefault(o.chan, [])
        lst.append(o)
        o.pos = len(lst)
        return o

    def plan(self):
        known = {e: {} for e in self.streams}
        snap = {}
        for o in self.ops:
            K = known[o.eng]
            need = {}
            for d, kind in o.deps.items():
                dop = self.ops[d]
                if dop.chan == o.chan and dop.chan == o.eng:
                    if o.eng == "pe" or (kind != "RAW" and not STRICT_SAME_ENGINE):
                        continue
                if need.get(dop.chan, 0) < dop.pos:
                    need[dop.chan] = dop.pos
            waits = []
            for chan, pos in need.items():
                if K.get(chan, 0) >= pos:
                    continue
                waits.append((chan, pos))
            for chan, pos in waits:
                tgt = self.chan_ops[chan][pos - 1]
                tgt.signal = True
                if K.get(chan, 0) < pos:
                    K[chan] = pos
                for c2, p2 in snap[tgt.id].items():
                    if K.get(c2, 0) < p2:
                        K[c2] = p2
            o.waits = waits
            s = dict(K)
            s[o.chan] = o.pos
            snap[o.id] = s
        for chan, lst in self.chan_ops.items():
            cnt = 0
            for o in lst:
                if o.signal:
                    cnt += o.inc
                o.sigval = cnt

    def emit(self):
        nc = self.nc
        self.plan()
        with ExitStack() as es:
            sems = {}
            for chan in self.chan_ops:
                if any(o.signal for o in self.chan_ops[chan]):
                    sems[chan] = es.enter_context(nc.semaphore("s_" + chan.replace(":", "_")))
            block = es.enter_context(nc.Block())

            def run(stream):
                def f(e):
                    for o in self.streams[stream]:
                        for chan, pos in o.waits:
                            e.wait_ge(sems[chan], self.chan_ops[chan][pos - 1].sigval)
                        ins = o.fn(e) if o.fn is not None else None
                        if o.signal and ins is not None:
                            ins.then_inc(sems[o.chan], o.inc)
                return f

            block.tensor(run("pe"))
            block.scalar(run("act"))
            block.vector(run("dve"))
            block.gpsimd(run("pool"))
            block.sync(run("sp"))


class Prog:
    def __init__(self, nc):
        self.nc = nc
        self.psum = []
        names = {}
        for b in range(8):
            t = nc.alloc_psum_tensor("psb%d" % b, [128, 512], F32)
            names["psb%d" % b] = b
            self.psum.append(t)
        self.S = Sched(nc, names)
        self.arena = nc.alloc_sbuf_tensor("arena", [128, ARENA_BYTES // 2], BF16)
        self.bank_rr = 0
        self.dma_n = 0

    def view(self, off, shape, dt):
        n = 1
        for s in shape:
            n *= s
        nb = n * ESZ[dt]
        assert off % 4 == 0 and off + nb <= ARENA_BYTES, (off, nb)
        ap = self.arena[:, off // 2:(off + nb) // 2]
        if dt != BF16:
            ap = ap.bitcast(dt)
        if len(shape) > 1:
            names = "abcdef"[:len(shape)]
            pat = "p (%s) -> p %s" % (" ".join(names), " ".join(names))
            ap = ap.rearrange(pat, **{names[i]: shape[i] for i in range(len(shape) - 1)})
        return ap

    def bank(self):
        b = self.bank_rr
        self.bank_rr = (b + 1) % 7
        return self.psum[b]

    def mm(self, out, lhsT, rhs, start=True, stop=True):
        return self.S.op("pe", lambda e: e.matmul(out, lhsT, rhs, start=start, stop=stop),
                         reads=[lhsT, rhs], writes=[out])

    def act(self, out, in_, func, scale=1.0, bias=0.0, eng="act"):
        rd = [in_]
        if not isinstance(scale, (int, float)):
            rd.append(scale)
        if not isinstance(bias, (int, float)):
            rd.append(bias)
        return self.S.op(eng, lambda e: e.activation(out, in_, func, bias=bias, scale=scale), reads=rd, writes=[out])

    def tt(self, out, in0, in1, op, eng="dve"):
        return self.S.op(eng, lambda e: e.tensor_tensor(out, in0, in1, op), reads=[in0, in1], writes=[out])

    def ts(self, out, in0, s1, s2, op0, op1=None, eng="dve"):
        rd = [in0]
        for s in (s1, s2):
            if s is not None and not isinstance(s, (int, float)):
                rd.append(s)
        if op1 is None:
            return self.S.op(eng, lambda e: e.tensor_scalar(out, in0, s1, None, op0), reads=rd, writes=[out])
        return self.S.op(eng, lambda e: e.tensor_scalar(out, in0, s1, s2, op0, op1), reads=rd, writes=[out])

    def stt(self, out, in0, scalar, in1, op0, op1, eng="dve"):
        rd = [in0, in1]
        if not isinstance(scalar, (int, float)):
            rd.append(scalar)
        return self.S.op(eng, lambda e: e.scalar_tensor_tensor(out, in0, scalar, in1, op0, op1), reads=rd, writes=[out])

    def copy(self, out, in_, eng="dve"):
        return self.S.op(eng, lambda e: e.tensor_copy(out, in_), reads=[in_], writes=[out])

    def memset(self, out, val, eng="dve"):
        return self.S.op(eng, lambda e: e.memset(out, val), reads=[], writes=[out])

    def dma(self, q, out, in_, slot=None):
        if slot is None:
            slot = "d%d" % self.dma_n
            self.dma_n += 1
        return self.S.op(q, lambda e: e.dma_start(out=out, in_=in_), reads=[in_], writes=[out],
                         chan="dma:" + slot, inc=16)


TILES = [(0, 4), (4, 7), (7, 10), (10, 13), (13, 17)]
OFF_X = 0
OFF_CX = 69632
OFF_H = 77824
OFF_SQ = 94208
OFF_TMP = 102400
OFF_RSTD = 106496
OFF_CONST = 110592
OFF_ADA = 115200
OFF_SCR = 123392


def build(mode="ALL"):
    nc = bass.Bass("TRN2", target_bir_lowering=False)
    P = Prog(nc)
    S = P.S
    do0 = mode in ("ALL", "L0")
    do1 = mode in ("ALL", "L1")

    def din(name, shape):
        return nc.dram_tensor(name, shape, F32, kind="ExternalInput").ap()

    def dout(name, shape):
        return nc.dram_tensor(name, shape, F32, kind="ExternalOutput").ap()

    d_xT = din("xT", [128, NCH, NT])
    d_ctxT = din("ctxT", [128, NCH, NCTX])
    d_cT = din("cT", [128, NCH * 2])
    d_small = din("small", [128, 16 + 256 + 48 + 96 + 16 + 128])
    d_ada = din("ada_r", [2, 24, 128, 2048])
    d_w1 = din("w1_r", [2, 16, 128, 2048])
    d_w2 = din("w2_r", [2, 8, 128, 4096])
    if do0:
        d_win = din("win_r", [128, 8192])
        d_wout = din("wout_r", [128, 8192])
        d_wgrp = din("wgrp_r", [128, 2048])
    if do1:
        d_wqkv = din("wqkv_r", [128, 8 * 1280])
        d_wo = din("wo_r", [128, 8192])
        d_rope = din("rope", [128, 2 * NT])
        d_mask = din("masks", [128, 384])
        d_out = dout("outT", [128, NCH, NT])
    else:
        d_x1 = dout("x1T", [128, NCH, NT])
        d_c1 = dout("c1T", [128, NCH, NCTX])

    X = P.view(OFF_X, [NCH, NT], F32)
    CX = P.view(OFF_CX, [NCH, NCTX], F32)
    co = [OFF_CONST]

    def calloc(shape, dt):
        n = 1
        for s_ in shape:
            n *= s_
        off = co[0]
        co[0] += (n * ESZ[dt] + 3) // 4 * 4
        assert co[0] <= OFF_ADA
        return P.view(off, shape, dt)

    SMALL = calloc([560], F32)
    PADV = SMALL[:, 0:16]
    CORR = SMALL[:, 16:272].rearrange("p (a b c d) -> p a b c d", a=2, b=2, c=8)
    GVEC = SMALL[:, 272:320].rearrange("p (a b) -> p a b", a=6)
    ADAB = SMALL[:, 320:416].rearrange("p (a b) -> p a b", a=2)
    SINK = SMALL[:, 416:432]
    XPAD = SMALL[:, 432:560].rearrange("p (a b) -> p a b", a=8)
    MOD = calloc([2, 48, 2], F32)
    AV = calloc([2, 2, 2, 8], F32)
    ONESM = calloc([128], BF16)
    ONES1 = calloc([128], BF16)
    CTF = calloc([16], F32)
    CTS = calloc([NCH, 2], BF16)
    ESK = calloc([16], F32)
    EPSV = calloc([1], F32)
    CORR1 = calloc([1], F32)

    cload = P.dma("sp", SMALL, d_small, slot="small")
    P.dma("sp", CTF, d_cT, slot="ct")
    P.memset(ONESM, 1.0 / 1024.0)
    P.memset(ONES1, 1.0)
    P.memset(EPSV, EPS)

    b0, b1 = TILES[0]
    P.dma("sp", X[:, :, b0 * 128:b1 * 128], d_xT[:, :, b0 * 128:b1 * 128])

    def late_x_loads(dep):
        for (b0, b1) in TILES[1:]:
            P.S.op("sp", (lambda b0=b0, b1=b1: (lambda e: e.dma_start(out=X[:, :, b0 * 128:b1 * 128], in_=d_xT[:, :, b0 * 128:b1 * 128])))(),
                   reads=[], writes=[X[:, :, b0 * 128:b1 * 128]], chan="dma:x%d" % b0, inc=16, extra_deps=dep)
        P.S.op("sp", lambda e: e.dma_start(out=CX, in_=d_ctxT), reads=[], writes=[CX], chan="dma:cx", inc=16, extra_deps=dep)

    P.act(CTS.rearrange("p a b -> p (a b)"), CTF, AF.Silu)

    hslot = [0]

    def Hview(N):
        k = hslot[0]
        hslot[0] ^= 1
        return P.view(OFF_H + k * 8192, [NCH, N], BF16)

    tslot = [0]

    def TMPv(N):
        k = tslot[0]
        tslot[0] ^= 1
        return P.view(OFF_TMP + k * 2048, [N], F32)

    rslot = [0]

    def RSv(N):
        k = rslot[0]
        rslot[0] ^= 1
        return P.view(OFF_RSTD + k * 2048, [N], F32)

    adaslot = [0]

    bgq = []
    if do0:
        bgq += [(0, j) for j in range(24)]
    if do1:
        bgq += [(1, j) for j in range(24)]
    bgs = {"dma": 0, "cmp": 0}
    mb = P.psum[7]

    def bg_slot(k):
        if k < 8:
            return OFF_SCR + k * 4096, "adas%d" % k
        return OFF_ADA + (k % 2) * 4096, "ada%d" % (k % 2)

    def bg_dma(k):
        l, j = bgq[k]
        off_, nm_ = bg_slot(k)
        slot2 = P.view(off_, [2048], BF16)
        P.dma("pool", slot2, d_ada[l, j], slot=nm_)

    def bg_cmp(k):
        l, j = bgq[k]
        slot3 = P.view(bg_slot(k)[0], [NCH, 256], BF16)
        for mi in range(2):
            m = 2 * j + mi
            for kc in range(NCH):
                P.mm(mb[:, 2 * m:2 * m + 2], slot3[:, kc, mi * 128:(mi + 1) * 128], CTS[:, kc, :],
                     start=(kc == 0), stop=(kc == NCH - 1))
        m0 = 2 * j
        op_ = P.tt(MOD[:, l, m0:m0 + 2, :], mb[:, 2 * m0:2 * m0 + 4].rearrange("p (a b) -> p a b", a=2),
                   ADAB[:, l, m0:m0 + 2].unsqueeze(2).to_broadcast([128, 2, 2]), ALU.add)
        if k == 6:
            late_x_loads([op_])
        for n in range(2):
            if j == (1 + 3 * n) * 4 + 3:
                for col in range(2):
                    P.stt(AV[:, l, n, col, :], MOD[:, l, (1 + 3 * n) * 8:(2 + 3 * n) * 8, col], 1.0,
                          GVEC[:, 2 * n + l, :], ALU.add, ALU.mult)

    def bg_step(n=1):
        for _ in range(n):
            k = bgs["cmp"]
            if k >= len(bgq):
                return
            nxt = k + 1
            if nxt >= 8 and nxt < len(bgq) and bgs["dma"] == nxt:
                bg_dma(nxt)
                bgs["dma"] += 1
            while bgs["dma"] <= k:
                bg_dma(bgs["dma"])
                bgs["dma"] += 1
            bg_cmp(k)
            bgs["cmp"] += 1

    def bg_ensure(l, which):
        tgt = bgq.index((l, which * 4 + 3)) + 1
        while bgs["cmp"] < tgt:
            bg_step()

    for k_ in range(8):
        bg_dma(k_)
    bgs["dma"] = 8

    def norm_sq(xsrc, N):
        sq = P.view(OFF_SQ, [NCH, N], BF16)
        P.act(sq, xsrc, AF.Square)
        return sq

    def norm_stat(sq, N, rs=None, bank=None):
        bk = bank if bank is not None else P.bank()
        for c in range(NCH):
            P.mm(bk[:, 0:N], ONESM, sq[:, c, :], start=(c == 0), stop=(c == NCH - 1))
        if rs is None:
            rs = RSv(N)
        P.act(rs, bk[:, 0:N], AF.Ln, bias=EPSV)
        P.act(rs, rs, AF.Exp, scale=-0.5)
        return rs

    def norm_apply(xsrc, N, rs, l, n, col, hdst, final=False):
        for c in range(NCH):
            if final:
                P.stt(hdst[:, c, :], xsrc[:, c, :], GVEC[:, 4, c:c + 1], rs, ALU.mult, ALU.mult)
                continue
            tmp = TMPv(N)
            P.tt(tmp, xsrc[:, c, :], rs, ALU.mult)
            if final:
                pass
            else:
                P.act(hdst[:, c, :], tmp, AF.Identity, scale=AV[:, l, n, col, c:c + 1],
                      bias=MOD[:, l, (3 * n) * 8 + c, col:col + 1])

    def norm_mod(xsrc, N, l, n, col, hdst, final=False):
        sq = norm_sq(xsrc, N)
        rs = norm_stat(sq, N)
        norm_apply(xsrc, N, rs, l, n, col, hdst, final)

    def norm_pieces(xsrc, N, l, n, col, hdst):
        stt_ = {}

        def p_sq():
            stt_["sq"] = norm_sq(xsrc, N)

        def p_st():
            stt_["rs"] = norm_stat(stt_["sq"], N)

        def p_ap(c):
            def f():
                tmp = TMPv(N)
                P.tt(tmp, xsrc[:, c, :], stt_["rs"], ALU.mult)
                P.act(hdst[:, c, :], tmp, AF.Identity, scale=AV[:, l, n, col, c:c + 1],
                      bias=MOD[:, l, (3 * n) * 8 + c, col:col + 1])
            return f
        return [p_sq, p_st] + [p_ap(c) for c in range(NCH)]

    def interleave(main, side, every=1, lead=0):
        side = list(side)
        for _ in range(min(lead, len(side))):
            side.pop(0)()
        for i_, m_ in enumerate(main):
            m_()
            if side and (i_ % every) == every - 1:
                side.pop(0)()
        while side:
            side.pop(0)()

    def mlp_prep(group):
        GT = sum(N for _, N, _ in group)
        assert GT <= 896
        offs = []
        o = 0
        for xv, N, col in group:
            offs.append(o)
            o += N
        return dict(group=group, GT=GT, offs=offs,
                    H2=P.view(OFF_H, [NCH, GT], BF16), HID=P.view(OFF_SCR, [NHC, GT], BF16))

    def mlp_norm(l, st):
        for (xv, N, col), o in zip(st["group"], st["offs"]):
            norm_mod(xv, N, l, 1, col, st["H2"][:, :, o:o + N])

    def mlp_1(l, st, hooks=None):
        o_w1 = OFF_SCR + NHC * 896 * 2
        H2, HID = st["H2"], st["HID"]
        for j in range(16):
            k = j % 3
            w2d = P.view(o_w1 + k * 4096, [2048], BF16)
            w3d = P.view(o_w1 + k * 4096, [NCH, 256], BF16)
            P.dma("pool", w2d, d_w1[l, j], slot="w1_%d" % k)
            for hh in range(2):
                hc = 2 * j + hh
                for (xv, N, col), o in zip(st["group"], st["offs"]):
                    bk = P.bank()
                    for kc in range(NCH):
                        P.mm(bk[:, 0:N], w3d[:, kc, hh * 128:(hh + 1) * 128], H2[:, kc, o:o + N],
                             start=(kc == 0), stop=(kc == NCH - 1))
                    tmp = TMPv(N)
                    P.act(tmp, bk[:, 0:N], AF.Relu)
                    P.tt(HID[:, hc, o:o + N], tmp, tmp, ALU.mult)
            if hooks and j in hooks:
                for f_ in hooks[j]:
                    f_()

    def mlp_2(l, st):
        o_w2 = OFF_SCR + NHC * 896 * 2 + 3 * 4096
        assert o_w2 + 2 * 8192 <= ARENA_BYTES
        HID = st["HID"]
        for fo in range(NCH):
            k = fo % 2
            w2d = P.view(o_w2 + k * 8192, [4096], BF16)
            w3d = P.view(o_w2 + k * 8192, [NHC, 128], BF16)
            P.dma("pool", w2d, d_w2[l, fo], slot="w2_%d" % k)
            for (xv, N, col), o in zip(st["group"], st["offs"]):
                bk = P.bank()
                for hc in range(NHC):
                    P.mm(bk[:, 0:N], w3d[:, hc, :], HID[:, hc, o:o + N], start=(hc == 0), stop=(hc == NHC - 1))
                P.stt(xv[:, fo, :], bk[:, 0:N], MOD[:, l, 40 + fo, col:col + 1], xv[:, fo, :], ALU.mult, ALU.add)
            bg_step()

    def staged_norm_hooks(l, st_next, hooks, first_j=1, bank=None):
        applies = []
        pend = {}
        tiles = list(zip(st_next["group"], st_next["offs"]))
        for ti_, ((xv, N, col), o) in enumerate(tiles):
            j_sq = first_j + 4 * ti_
            j_st = first_j + 4 * ti_ + 4

            def f_sq(xv=xv, N=N, ti_=ti_):
                pend[ti_] = norm_sq(xv, N)

            def f_st(N=N, ti_=ti_):
                pend[ti_] = norm_stat(pend[ti_], N, bank=bank)

            hooks.setdefault(j_sq, []).append(f_sq)
            hooks.setdefault(j_st, []).insert(0, f_st)

            def f_ap(xv=xv, N=N, col=col, o=o, ti_=ti_):
                norm_apply(xv, N, pend[ti_], l, 1, col, st_next["H2"][:, :, o:o + N])

            applies.append(f_ap)
        return applies

    def mlp_layer(l, groups, final_fn=None, first_staged=None):
        sts = [mlp_prep(g) for g in groups]
        if first_staged is None:
            mlp_norm(l, sts[0])
        else:
            for f_ in first_staged:
                f_()
        for gi, st in enumerate(sts):
            hooks = {}
            applies = []
            if gi + 1 < len(sts):
                applies = staged_norm_hooks(l, sts[gi + 1], hooks)
            if final_fn is not None and gi > 0:
                final_fn(gi - 1, hooks)
            mlp_1(l, st, hooks)
            for f_ in applies:
                f_()
            mlp_2(l, st)
        if final_fn is not None:
            final_fn(len(sts) - 1, None)

    lat_tiles = [(X[:, :, b0 * 128:b1 * 128], (b1 - b0) * 128, 0) for (b0, b1) in TILES]

    if do0:
        bg_ensure(0, 1)
        o = OFF_SCR
        UL = P.view(o, [NCH, NT + 16], BF16); o += NCH * (NT + 16) * 2
        UC = P.view(o, [NCH, NCTX + 16], BF16); o += NCH * (NCTX + 16) * 2
        o_win = o
        WINU = [P.view(o + u_ * 4096, [NCH, 256], BF16) for u_ in range(4)]
        WINU2 = [P.view(o + u_ * 4096, [2048], BF16) for u_ in range(4)]
        DTS = [P.view(o, [NCH, 512], BF16), P.view(OFF_H, [NCH, 512], BF16)]
        Y1S = [P.view(o + 8192, [NCH, 512], BF16), P.view(OFF_H + 8192, [NCH, 512], BF16)]
        o += 16384
        WOUT2 = P.view(o, [8192], BF16); WOUT = P.view(o, [NCH, 1024], BF16); o += 16384
        WG2 = P.view(o, [2048], BF16); WG = P.view(o, [4, 2, 256], BF16); o += 4096
        WGS = P.view(o, [4, 2, 256], BF16); o += 4096
        PT_ = [P.view(o + i * 2112, [2, 528], BF16) for i in range(3)]; o += 3 * 2112
        PTP_ = [P.view(o + i * 2112, [2, 528], BF16) for i in range(1)]; o += 1 * 2112
        assert o <= ARENA_BYTES, o
        for u_ in range(4):
            P.dma("pool", WINU2[u_], d_win[:, u_ * 2048:(u_ + 1) * 2048], slot="win%d" % u_)
        P.memset(UC[:, :, 0:8], 0.0)
        P.memset(UC[:, :, NCTX + 8:NCTX + 16], 0.0)

        ph1 = [(xv, N, col, UL, 8 + b0 * 128) for (xv, N, col), (b0, b1) in zip(lat_tiles, TILES)]
        ph1.insert(1, (XPAD, 16, 0, UL, None))
        ph1.append((CX, NCTX, 1, UC, 8))
        hs = {}

        def p1_norm_pieces(i):
            xv, N, col, UB, ucol = ph1[i]
            hs[i] = Hview(N)
            return norm_pieces(xv, N, 0, 0, col, hs[i])

        def p1_proj_pieces(i):
            xv, N, col, UB, ucol = ph1[i]
            h = hs[i]

            def piece(cc):
                def f():
                    bk = P.bank()
                    for kc in range(NCH):
                        P.mm(bk[:, 0:N], WINU[cc // 2][:, kc, (cc % 2) * 128:(cc % 2) * 128 + 128], h[:, kc, :], start=(kc == 0), stop=(kc == NCH - 1))
                    if ucol is None:
                        P.tt(UB[:, cc, 0:8], bk[:, 0:8], PADV[:, 0:8], ALU.mult)
                        P.tt(UB[:, cc, NT + 8:NT + 16], bk[:, 8:16], PADV[:, 8:16], ALU.mult)
                    else:
                        P.copy(UB[:, cc, ucol:ucol + N], bk[:, 0:N])
                return f
            return [piece(cc) for cc in range(NCH)]

        for f_ in p1_norm_pieces(0):
            f_()
        for i in range(len(ph1)):
            side = p1_norm_pieces(i + 1) if i + 1 < len(ph1) else []
            interleave(p1_proj_pieces(i), side, lead=2)
            bg_step()
            if i == 1:
                P.dma("pool", WG2, d_wgrp, slot="wg")
                P.dma("pool", WOUT2, d_wout, slot="wout")
                for g_, w_ in enumerate((2, 4, 8, 16)):
                    P.ts(WGS[:, g_], WG[:, g_], 1.0 / w_, None, ALU.mult)
                P.ts(WG2, WG2, -1.0, None, ALU.mult)
        bg_ensure(0, 2)

        ph2 = [(xv, N, col, UL, 8 + b0 * 128, 0, b0 == 0, b1 == NBLK) for (xv, N, col), (b0, b1) in zip(lat_tiles, TILES)]
        ph2.append((CX, NCTX, 1, UC, 8, 1, True, True))

        def p2_pool_g(i, g, w):
            xv, N, col, UB, base, ci, first, last = ph2[i]
            DT = DTS[i % 2]
            if True:
                cs = slice(2 * g, 2 * g + 2)
                en = "pool" if g < 2 else "dve"
                if g < 2:
                    A_ = PTP_[0]
                    B_ = C_ = None
                else:
                    A_, B_, C_ = PT_
                dst = DT[:, cs, 0:N]

                def u(a, n):
                    return UB[:, cs, base + a:base + a + n]
                if w == 2:
                    P.tt(dst, u(-1, N), u(0, N), ALU.add, eng=en)
                elif w == 4:
                    P.tt(A_[:, :, 0:N + 2], u(-2, N + 2), u(-1, N + 2), ALU.add, eng=en)
                    P.tt(dst, A_[:, :, 0:N], A_[:, :, 2:N + 2], ALU.add, eng=en)
                elif w == 8:
                    P.tt(A_[:, :, 0:N + 6], u(-4, N + 6), u(-3, N + 6), ALU.add, eng=en)
                    P.tt(B_[:, :, 0:N + 4], A_[:, :, 0:N + 4], A_[:, :, 2:N + 6], ALU.add, eng=en)
                    P.tt(dst, B_[:, :, 0:N], B_[:, :, 4:N + 4], ALU.add, eng=en)
                else:
                    P.tt(A_[:, :, 0:N + 14], u(-8, N + 14), u(-7, N + 14), ALU.add, eng=en)
                    P.tt(B_[:, :, 0:N + 12], A_[:, :, 0:N + 12], A_[:, :, 2:N + 14], ALU.add, eng=en)
                    P.tt(C_[:, :, 0:N + 8], B_[:, :, 0:N + 8], B_[:, :, 4:N + 12], ALU.add, eng=en)
                    P.tt(dst, C_[:, :, 0:N], C_[:, :, 8:N + 8], ALU.add, eng=en)
                if first:
                    P.tt(DT[:, cs, 0:8], DT[:, cs, 0:8], CORR[:, ci, 0, cs, :], ALU.mult, eng=en)
                if last:
                    P.tt(DT[:, cs, N - 8:N], DT[:, cs, N - 8:N], CORR[:, ci, 1, cs, :], ALU.mult, eng=en)

        def p2_pool_pieces(i):
            return [(lambda g=g, w=w: p2_pool_g(i, g, w)) for g, w in enumerate((2, 4, 8, 16))]

        def p2_mm_pieces(i):
            xv, N, col, UB, base, ci, first, last = ph2[i]
            DT = DTS[i % 2]
            Y1 = Y1S[i % 2]

            def grp(cc):
                def f():
                    g = cc // 2
                    bk = P.bank()
                    cw = slice((cc % 2) * 128, (cc % 2) * 128 + 128)
                    for k2 in range(2):
                        P.mm(bk[:, 0:N], WGS[:, g, k2, cw], DT[:, 2 * g + k2, 0:N], start=(k2 == 0), stop=False)
                    for k2 in range(2):
                        P.mm(bk[:, 0:N], WG[:, g, k2, cw], UB[:, 2 * g + k2, base:base + N], start=False, stop=(k2 == 1))
                    P.act(Y1[:, cc, 0:N], bk[:, 0:N], AF.Copy, scale=GVEC[:, 5, cc:cc + 1])
                return f

            def wout(cc):
                def f():
                    bk = P.bank()
                    for kc in range(NCH):
                        P.mm(bk[:, 0:N], WOUT[:, kc, cc * 128:(cc + 1) * 128], Y1[:, kc, 0:N], start=(kc == 0), stop=(kc == NCH - 1))
                    P.stt(xv[:, cc, :], bk[:, 0:N], MOD[:, 0, 16 + cc, col:col + 1], xv[:, cc, :], ALU.mult, ALU.add)
                return f
            return [grp(cc) for cc in range(NCH)] + [wout(cc) for cc in range(NCH)]

        ctx_tile = (CX, NCTX, 1)
        l0_groups = [[lat_tiles[0], lat_tiles[1]], [lat_tiles[2], lat_tiles[3]], [lat_tiles[4], ctx_tile]]
        st0 = mlp_prep(l0_groups[0])
        hk0 = {}
        first_applies = staged_norm_hooks(0, st0, hk0, first_j=3)
        for f_ in p2_pool_pieces(0):
            f_()
        stage_seq = []
        for j_ in sorted(hk0):
            stage_seq += hk0[j_]
        for i in range(len(ph2)):
            side = p2_pool_pieces(i + 1) if i + 1 < len(ph2) else []
            main = p2_mm_pieces(i)
            interleave(main[:8], side[:2], every=4)
            bg_step()
            interleave(main[8:], side[2:], every=4)
            bg_step()
            if i >= 2 and stage_seq:
                stage_seq.pop(0)()

        while stage_seq:
            stage_seq.pop(0)()
        bg_ensure(0, 5)
        mlp_layer(0, l0_groups, first_staged=first_applies)
        if do1:
            bg_ensure(1, 5)
    elif do1:
        bg_ensure(1, 5)

    if do1:
        o = OFF_ADA
        QT = P.view(o, [NCH, NT], BF16); o += NCH * NT * 2
        KTK = [P.view(o + i * (NT + NCTX) * 2, [NT + NCTX], BF16) for i in range(2)]; o += 2 * (NT + NCTX) * 2
        VA = [P.view(o + i * (NBLK + 2) * 256, [NBLK + 2, 128], BF16) for i in range(2)]; o += 2 * (NBLK + 2) * 256
        o_ph = o
        ROPE2 = P.view(o, [2 * NT], BF16); COS = P.view(o, [NT], BF16); SIN = P.view(o + NT * 2, [NT], BF16); o += 4 * NT
        WV = P.view(o, [NCH, 128], BF16); WV2 = P.view(o, [1024], BF16)
        WQC = [P.view(o + 2048 + c * 2048, [NCH, 128], BF16) for c in range(9)]
        WQC2 = [P.view(o + 2048 + c * 2048, [1024], BF16) for c in range(9)]
        o += 10 * 2048
        PERM = P.view(o, [128], BF16); o += 256
        QB = [P.view(o + i * 1024, [512], BF16) for i in range(2)]; o += 2048
        RTS = [P.view(o + i * 2048, [512], F32) for i in range(4)]; o += 8192
        rtc = [0]
        assert o <= ARENA_BYTES, o
        P.dma("pool", WQC2[0], d_wqkv[:, 1024:2048], slot="wq0")
        P.dma("pool", PERM, d_mask[:, 256:384], slot="perm")
        P.dma("pool", ROPE2, d_rope, slot="rope")
        for c in range(1, 9):
            P.dma("pool", WQC2[c], d_wqkv[:, 1024 + c * 1024:1024 + (c + 1) * 1024], slot="wq%d" % c)
        P.dma("pool", WV2, d_wqkv[:, 0:1024], slot="wv")
        P.memset(KTK[0][64:128, :], 0.0)
        P.memset(VA[0][:, :, 64:128], 1.0)
        P.memset(VA[1][:, :, 0:64], 1.0)
        P.memset(KTK[1][0:64, :], 0.0)
        P.act(ESK, SINK, AF.Exp)

        l1t = [(xv, N, 0, b0, b1) for (xv, N, col), (b0, b1) in zip(lat_tiles, TILES)] + [(CX, NCTX, 1, NBLK, NBLK + 2)]
        hs1 = {}

        def q_norm_pieces(i):
            xv, N, col, b0, b1 = l1t[i]
            hs1[i] = Hview(N)
            return norm_pieces(xv, N, 1, 0, col, hs1[i])

        def q_proj_pieces(i):
            xv, N, col, b0, b1 = l1t[i]
            h = hs1[i]
            t0 = b0 * 128
            pcs = []
            pend = {}

            def rope_fin(c, ba, qb_):
                bb = P.bank()
                P.mm(bb[:, 0:N], PERM, qb_[:, 0:N])
                t1 = RTS[rtc[0] % 4][:, 0:N]
                t2 = RTS[(rtc[0] + 1) % 4][:, 0:N]
                rtc[0] += 2
                P.tt(t1, ba[:, 0:N], COS[:, t0:t0 + N], ALU.mult)
                P.tt(t2, bb[:, 0:N], SIN[:, t0:t0 + N], ALU.mult)
                if c < NCH:
                    P.tt(QT[:, c, t0:t0 + N], t1, t2, ALU.add, eng="pool")
                else:
                    P.tt(KTK[0][0:64, t0:t0 + N], t1[0:64], t2[0:64], ALU.add, eng="pool")
                    P.tt(KTK[1][64:128, t0:t0 + N], t1[64:128], t2[64:128], ALU.add, eng="pool")

            def vpiece():
                nb = b1 - b0
                bv = P.bank()
                for bi in range(nb):
                    for kc in range(NCH):
                        P.mm(bv[:, bi * 128:(bi + 1) * 128], h[:, kc, bi * 128:(bi + 1) * 128], WV[:, kc, :],
                             start=(kc == 0), stop=(kc == NCH - 1))
                bv3 = bv[:, 0:nb * 128].rearrange("p (a b) -> p a b", a=nb)
                P.act(VA[0][:, b0:b1, 0:64], bv3[:, :, 0:64], AF.Copy)
                P.act(VA[1][:, b0:b1, 64:128], bv3[:, :, 64:128], AF.Copy)

            if col == 0:
                def qpiece(c):
                    def f():
                        ba = P.bank()
                        for kc in range(NCH):
                            P.mm(ba[:, 0:N], WQC[c][:, kc, :], h[:, kc, :], start=(kc == 0), stop=(kc == NCH - 1))
                        qb_ = QB[c % 2]
                        P.act(qb_[:, 0:N], ba[:, 0:N], AF.Copy)
                        if "p" in pend:
                            rope_fin(*pend["p"])
                        pend["p"] = (c, ba, qb_)
                    return f
                pcs = [qpiece(c) for c in range(NCH + 1)]

                def last():
                    rope_fin(*pend["p"])
                    vpiece()
                pcs.append(last)
            else:
                def kpiece():
                    ba = P.bank()
                    for kc in range(NCH):
                        P.mm(ba[:, 0:N], WQC[8][:, kc, :], h[:, kc, :], start=(kc == 0), stop=(kc == NCH - 1))
                    P.act(KTK[0][0:64, NT:NT + NCTX], ba[0:64, 0:N], AF.Copy)
                    P.act(KTK[1][64:128, NT:NT + NCTX], ba[64:128, 0:N], AF.Copy)
                pcs = [kpiece, vpiece]
            return pcs

        for f_ in q_norm_pieces(0):
            f_()
        for i in range(len(l1t)):
            side = q_norm_pieces(i + 1) if i + 1 < len(l1t) else []
            interleave(q_proj_pieces(i), side)

        o = o_ph
        WO2 = P.view(o, [8192], BF16); WO = P.view(o, [NCH, 1024], BF16); o += 16384
        PTS = [P.view(o + i * 1024, [4, 128], BF16) for i in range(10)]; o += 10 * 1024
        OTS = [P.view(OFF_H + i * 8192, [NCH, 512], BF16) for i in range(2)]
        DNS = [P.view(o + i * 2048, [4, 128], F32) for i in range(3)]; o += 3 * 2048
        RC2 = [P.view(o + i * 2048, [4, 128], F32) for i in range(3)]; o += 3 * 2048
        MSK2 = P.view(o, [256], BF16); MSK = P.view(o, [2, 128], BF16); o += 512
        assert o <= ARENA_BYTES
        P.dma("pool", MSK2, d_mask[:, 0:256], slot="mask")
        P.dma("pool", WO2, d_wo, slot="wo")
        SBK = [P.psum[i] for i in range(3)]
        OBK = [P.psum[4 + i] for i in range(4)]
        sctr = [0]

        units = [(qb, kvh, half) for qb in range(NBLK) for half in range(2) for kvh in range(2)]
        tile_of = {}
        for ti, (b0, b1) in enumerate(TILES):
            for qb in range(b0, b1):
                tile_of[qb] = ti

        def kbs_of(qb):
            kbs = []
            if qb > 0:
                kbs.append((qb - 1, 0))
            kbs.append((qb, None))
            if qb < NBLK - 1:
                kbs.append((qb + 1, 1))
            kbs.append((NBLK, None))
            kbs.append((NBLK + 1, None))
            return kbs

        def emit_S(ui, i):
            qb, kvh, half = units[ui]
            c0 = half * 4
            kb, mk = kbs_of(qb)[i]
            sb = SBK[sctr[0] % 3]
            sctr[0] += 1
            P.mm(sb[:, :], KTK[kvh][:, kb * 128:(kb + 1) * 128], QT[:, c0:c0 + 4, qb * 128:(qb + 1) * 128])
            pt = PTS[(ui % 2) * 5 + i]
            P.act(pt, sb[:, :].rearrange("p (a b) -> p a b", a=4), AF.Exp, scale=0.125)
            if mk is not None:
                P.tt(pt, pt, MSK[:, mk, :].unsqueeze(1).to_broadcast([128, 4, 128]), ALU.mult, eng="pool")

        def emit_PV(ui, i):
            qb, kvh, half = units[ui]
            kbs = kbs_of(qb)
            n = len(kbs)
            kb, mk = kbs[i]
            ob = OBK[ui % 4]
            pt = PTS[(ui % 2) * 5 + i].rearrange("p a b -> p (a b)")
            P.mm(ob[:, :], VA[kvh][:, kb, :], pt, start=(i == 0), stop=(i == n - 1))

        def emit_den(ui):
            qb, kvh, half = units[ui]
            c0 = half * 4
            drows = slice(64, 128) if kvh == 0 else slice(0, 64)
            ob = OBK[ui % 4]
            dn = DNS[(ui // 2) % 3]
            h0 = kvh * 8 + c0
            P.tt(dn[drows], ob[drows, :].rearrange("p (a b) -> p a b", a=4),
                 ESK[drows, h0:h0 + 4].unsqueeze(2).to_broadcast([64, 4, 128]), ALU.add)

        def emit_recip(pi):
            dn = DNS[pi % 3]
            r2 = RC2[pi % 3]
            P.act(dn, dn, AF.Ln)
            P.act(dn, dn, AF.Exp, scale=-1.0)
            P.dma("sp", r2[0:64], dn[64:128], slot="rsw%da" % (pi % 3))
            P.dma("sp", r2[64:128], dn[0:64], slot="rsw%db" % (pi % 3))

        def emit_mult(ui):
            qb, kvh, half = units[ui]
            rows = slice(kvh * 64, kvh * 64 + 64)
            c0 = half * 4
            ti = tile_of[qb]
            b0, b1 = TILES[ti]
            qoff = (qb - b0) * 128
            ob = OBK[ui % 4]
            r2 = RC2[(ui // 2) % 3]
            P.tt(OTS[ti % 2][rows, c0:c0 + 4, qoff:qoff + 128], ob[rows, :].rearrange("p (a b) -> p a b", a=4),
                 r2[rows], ALU.mult)

        deferred = []

        def wo_closure(ti, fo):
            def f():
                xv, N, col = lat_tiles[ti]
                bk = P.psum[3]
                for c in range(NCH):
                    P.mm(bk[:, 0:N], WO[:, c, fo * 128:(fo + 1) * 128], OTS[ti % 2][:, c, 0:N], start=(c == 0), stop=(c == NCH - 1))
                P.stt(xv[:, fo, :], bk[:, 0:N], MOD[:, 1, 16 + fo, 0:1], xv[:, fo, :], ALU.mult, ALU.add)
            return f

        def finish_pair(pi):
            ua, ub = 2 * pi, 2 * pi + 1
            emit_mult(ua)
            emit_mult(ub)
            qb = units[ub][0]
            ti = tile_of[qb]
            b0, b1 = TILES[ti]
            if qb == b1 - 1 and units[ub][2] == 1:
                for fo in range(NCH):
                    deferred.append(wo_closure(ti, fo))

        L1M = [(0, 3), (3, 6), (6, 9), (9, 12), (12, 15), (15, 17)]
        m_tiles = [(X[:, :, b0 * 128:b1 * 128], (b1 - b0) * 128, 0) for (b0, b1) in L1M]
        l1_groups = [[m_tiles[0], m_tiles[1]], [m_tiles[2], m_tiles[3]], [m_tiles[4], m_tiles[5]]]
        st1 = mlp_prep(l1_groups[0])
        att_hooks = {}
        l1_first_applies = staged_norm_hooks(1, st1, att_hooks, first_j=40, bank=P.psum[3])

        for i in range(len(kbs_of(units[0][0]))):
            emit_S(0, i)
        for ui, (qb, kvh, half) in enumerate(units):
            nP = len(kbs_of(qb))
            nS = len(kbs_of(units[ui + 1][0])) if ui + 1 < len(units) else 0
            for i in range(max(nP, nS)):
                if i < nS:
                    emit_S(ui + 1, i)
                if i < nP:
                    emit_PV(ui, i)
            emit_den(ui)
            if ui % 2 == 1:
                emit_recip(ui // 2)
            elif ui >= 2:
                finish_pair(ui // 2 - 1)
            if deferred:
                deferred.pop(0)()
            for f_ in att_hooks.get(ui, []):
                f_()
        finish_pair(len(units) // 2 - 1)
        while deferred:
            deferred.pop(0)()
        P.bank_rr = 0

        outs = []
        fin = {0: [0, 1], 1: [2, 3], 2: [4, 5]}
        FRS = [P.view(OFF_ADA + i * 2048, [512], F32) for i in range(2)]

        def final_fn(gi, hooks):
            tis = fin[gi]
            pend = {}
            for k_, ti in enumerate(tis):
                xv, N, col = m_tiles[ti]
                b0, b1 = L1M[ti]

                def f_sq(xv=xv, N=N, k_=k_):
                    pend[k_] = norm_sq(xv, N)

                def f_fin(xv=xv, N=N, k_=k_, b0=b0, b1=b1):
                    rs = norm_stat(pend[k_], N, rs=FRS[k_][:, 0:N])
                    norm_apply(xv, N, rs, 0, 0, 0, xv, final=True)
                    outs.append(P.dma("sp", d_out[:, :, b0 * 128:b1 * 128], xv))

                if hooks is None:
                    f_sq()
                    f_fin()
                else:
                    hooks.setdefault(10 + 3 * k_, []).append(f_sq)
                    hooks.setdefault(12 + 3 * k_, []).append(f_fin)

        mlp_layer(1, l1_groups, final_fn=final_fn, first_staged=l1_first_applies)
        S.op("sp", None, extra_deps=outs)
    else:
        outs = []
        for (xv, N, col), (b0, b1) in zip(lat_tiles, TILES):
            outs.append(P.dma("sp", d_x1[:, :, b0 * 128:b1 * 128], xv))
        outs.append(P.dma("sp", d_c1, CX))
        S.op("sp", None, extra_deps=outs)

    S.emit()
    return nc


POOL_WINDOWS = (2, 4, 8, 16)


def _fm(a):
    T = a.shape[0]
    return np.ascontiguousarray(a.reshape(T, NCH, 128).transpose(2, 1, 0))


def _wl(w):
    K, N = w.shape
    return np.ascontiguousarray(w.reshape(K // 128, 128, N).transpose(1, 0, 2).reshape(128, (K // 128) * N))


def _corr_tables(t0g, n_main, L):
    c = np.ones((2, NCH, 8), np.float32)
    for g, w in enumerate(POOL_WINDOWS):
        for side in range(2):
            for i in range(8):
                t = t0g + i if side == 0 else t0g + n_main - 8 + i
                lo = min(max(t - w // 2, 0), L)
                hi = min(max(t + w // 2, 0), L)
                c[side, 2 * g:2 * g + 2, i] = np.float32(w) / np.float32(hi - lo)
    return c


def _rope_tables(t0g):
    t = np.arange(t0g, t0g + NT)
    row = (t // 64).astype(np.float32)
    colp = (t % 64).astype(np.float32)
    inv = (np.float32(10000.0) ** (-np.arange(0, 32, 2, dtype=np.float32) / np.float32(32))).astype(np.float32)
    cos = np.zeros((128, NT), np.float32)
    sin = np.zeros((128, NT), np.float32)
    for p in range(128):
        d = p % 64
        pos = row if d < 32 else colp
        dd = d % 32
        ang = (pos * inv[dd % 16]).astype(np.float32)
        cos[p] = np.cos(ang)
        sin[p] = -np.sin(ang) if dd < 16 else np.sin(ang)
    return np.concatenate([cos, sin], axis=1)


def _shared_weights(inp):
    f = np.float32
    sh = {}
    ada_w = np.asarray(inp["ada_w"], f)
    sh["ada_r"] = np.ascontiguousarray(ada_w.reshape(2, NCH, 128, 24, 256).transpose(0, 3, 2, 1, 4).reshape(2, 24, 128, 2048))
    w1 = np.asarray(inp["mlp_w1"], f)
    sh["w1_r"] = np.ascontiguousarray(w1.reshape(2, NCH, 128, 16, 256).transpose(0, 3, 2, 1, 4).reshape(2, 16, 128, 2048))
    w2 = np.asarray(inp["mlp_w2"], f)
    sh["w2_r"] = np.ascontiguousarray(w2.reshape(2, NHC, 128, NCH, 128).transpose(0, 3, 2, 1, 4).reshape(2, NCH, 128, 4096))
    win = np.asarray(inp["pool_w_in"], f)[0]
    sh["win_r"] = np.concatenate([_wl(win[:, u * 256:(u + 1) * 256]) for u in range(4)], axis=1)
    sh["wout_r"] = _wl(np.asarray(inp["pool_w_out"], f)[0])
    wg = np.asarray(inp["pool_w_grp"], f)[0]
    sh["wgrp_r"] = np.ascontiguousarray(wg.reshape(4, 2, 128, 256).transpose(2, 0, 1, 3).reshape(128, 2048))
    wqkv = np.asarray(inp["attn_w_qkv"], f)[0]
    hd_order = []
    for c in range(8):
        hd_order += [c, 8 + c]
    qcols = np.concatenate([np.arange(h * 64, h * 64 + 64) for h in hd_order])
    partner = np.array([(d + 16) if (d % 32) < 16 else (d - 16) for d in range(64)])
    qpcols = np.concatenate([h * 64 + partner for h in hd_order])
    kcols = 1024 + np.arange(128)
    kpcols = 1024 + np.concatenate([h * 64 + partner for h in range(2)])
    vcols = 1152 + np.arange(128)
    units = [vcols]
    for c in range(8):
        units.append(qcols[c * 128:(c + 1) * 128])
    units.append(kcols)
    sh["wqkv_r"] = np.concatenate([_wl(wqkv[:, u]) for u in units], axis=1)
    wo = np.asarray(inp["attn_w_o"], f)[0]
    sh["wo_r"] = _wl(wo[qcols, :])
    ml = np.zeros((128, 2, 128), f)
    jj = np.arange(128)[:, None]
    ii = np.arange(128)[None, :]
    ml[:, 0, :] = (jj >= ii)
    ml[:, 1, :] = (jj <= ii)
    pm = np.zeros((128, 128), f)
    for m in range(128):
        pm[(m // 64) * 64 + partner[m % 64], m] = 1.0
    sh["masks"] = np.concatenate([ml.reshape(128, 256), pm], axis=1)
    return sh


def _core_inputs(inp, core):
    f = np.float32
    b, hf = core // 2, core % 2
    x = np.asarray(inp["x"], f)[b]
    t0g = 0 if hf == 0 else SEQ - NT
    d = {}
    d["xT"] = _fm(x[t0g:t0g + NT])
    d["ctxT"] = _fm(np.asarray(inp["ctx"], f)[b])
    cc = np.stack([np.asarray(inp["c"], f)[b], np.asarray(inp["c_ctx"], f)], axis=1)
    d["cT"] = np.ascontiguousarray(cc.reshape(NCH, 128, 2).transpose(1, 0, 2).reshape(128, 16))
    small = np.zeros((128, 560), f)
    xpad = np.zeros((16, D), f)
    if hf == 0:
        small[:, 8:16] = 1.0
        xpad[8:16] = x[NT:NT + 8]
    else:
        small[:, 0:8] = 1.0
        xpad[0:8] = x[t0g - 8:t0g]
    corr = np.stack([_corr_tables(t0g, NT, SEQ), _corr_tables(0, NCTX, NCTX)], axis=0)
    small[:, 16:272] = corr.reshape(1, 256)
    gv = np.stack([np.asarray(inp["norm_mix_g"], f)[0], np.asarray(inp["norm_mix_g"], f)[1],
                   np.asarray(inp["norm_mlp_g"], f)[0], np.asarray(inp["norm_mlp_g"], f)[1],
                   np.asarray(inp["final_g"], f), np.asarray(inp["pool_scale"], f)[0]], axis=0)
    small[:, 272:320] = gv.reshape(6, NCH, 128).transpose(2, 0, 1).reshape(128, 48)
    ab = np.asarray(inp["ada_b"], f)
    small[:, 320:416] = ab.reshape(2, 48, 128).transpose(2, 0, 1).reshape(128, 96)
    small[:, 416:432] = np.asarray(inp["attn_sink"], f)[0][None, :]
    small[:, 432:560] = _fm(xpad).reshape(128, 128)
    d["small"] = small
    d["rope"] = _rope_tables(t0g)
    return d


def _run(mode, shared, percore):
    nc = build(mode)
    names_by_mode = {
        "ALL": ["ada_r", "w1_r", "w2_r", "win_r", "wout_r", "wgrp_r", "wqkv_r", "wo_r", "masks"],
        "L0": ["ada_r", "w1_r", "w2_r", "win_r", "wout_r", "wgrp_r"],
        "L1": ["ada_r", "w1_r", "w2_r", "wqkv_r", "wo_r", "masks"],
    }[mode]
    in_maps = []
    for c in range(8):
        m = {k: shared[k] for k in names_by_mode}
        for k, v in percore[c].items():
            if k == "rope" and mode == "L0":
                continue
            m[k] = v
        in_maps.append(m)
    return run_bass_kernel_spmd(nc, in_maps, core_ids=list(range(8))).results


FUSED = True


def kernel(**inputs):
    shared = _shared_weights(inputs)
    percore = [_core_inputs(inputs, c) for c in range(8)]
    if FUSED:
        res = _run("ALL", shared, percore)
    else:
        r0 = _run("L0", shared, percore)
        for c in range(8):
            percore[c]["xT"] = r0[c]["x1T"]
            percore[c]["ctxT"] = r0[c]["c1T"]
        res = _run("L1", shared, percore)
    out = np.zeros((NB, SEQ, D), np.float32)
    for c in range(8):
        b, hf = c // 2, c % 2
        o = res[c]["outT"]
        tok = o.transpose(2, 1, 0).reshape(NT, D)
        if hf == 0:
            out[b, 0:2048] = tok[0:2048]
        else:
            out[b, 2048:4096] = tok[NT - 2048:NT]
    return out
```
